# Optimizing a Trainium2 kernel written in Bass

```python
import jax, jax.numpy as jnp
from jax import lax
import numpy as np

D_MODEL = 2048
BATCH = 2
SEQ = 8192
DEPTH = 4

CHUNK = 64
EPS = 1e-6
MIN_FORGET = 1e-30
A_VAL = 128
A_KEY = 128
A_VW = D_MODEL // 2
A_HEADS = A_VW // A_VAL
A_KW = A_HEADS * A_KEY
B_WIDTH = D_MODEL // 2
B_WINDOWS = (2, 4, 8, 16)
B_GROUPS = len(B_WINDOWS)
B_GW = B_WIDTH // B_GROUPS
IN_SIZES = (A_KW, A_KW, A_VW, A_VW, B_WIDTH, B_WIDTH, D_MODEL, D_MODEL)
IN_COLS = sum(IN_SIZES)
IN_SPLITS = [int(v) for v in np.cumsum(IN_SIZES)[:-1]]

kernel_name = "hybrid_hgrn2_pool_gated_trunk"


def rmsnorm(x, gain):
    xf = x.astype(jnp.float32)
    y = xf * lax.rsqrt(jnp.mean(xf * xf, axis=-1, keepdims=True) + EPS)
    return y.astype(x.dtype) * gain


def hgrn2_mixer(q_raw, f_raw, v, lb):
    f32 = jnp.float32
    bsz, seq, _ = q_raw.shape
    nc = seq // CHUNK
    q = jax.nn.silu(q_raw.astype(f32))
    a = f_raw.astype(f32)
    lb = lb.astype(f32)
    f = lb + (1.0 - lb) * jax.nn.sigmoid(a)
    log_f = jnp.log(jnp.maximum(f, MIN_FORGET))
    k = (1.0 - lb) * jax.nn.sigmoid(-a)
    v = v.astype(f32)

    def to_chunks(t, d):
        return t.reshape(bsz, nc, CHUNK, A_HEADS, d).transpose(1, 0, 3, 2, 4)

    qc, kc, lfc = to_chunks(q, A_KEY), to_chunks(k, A_KEY), to_chunks(log_f, A_KEY)
    vc = to_chunks(v, A_VAL)
    causal = jnp.tril(jnp.ones((CHUNK, CHUNK), dtype=bool))[:, :, None]
    causal_f = causal.astype(f32)

    def step(state, inp):
        qi, ki, vi, lfi = inp
        b = jnp.cumsum(lfi, axis=2)
        o_inter = jnp.einsum('bhtk,bhkv->bhtv', qi * jnp.exp(b), state)
        diff = b[:, :, :, None, :] - b[:, :, None, :, :]
        decay = jnp.exp(jnp.where(causal, diff, 0.0)) * causal_f
        scores = jnp.einsum('bhtk,bhtsk,bhsk->bhts', qi, decay, ki)
        o = o_inter + jnp.einsum('bhts,bhsv->bhtv', scores, vi)
        b_last = b[:, :, -1:, :]
        state = (jnp.exp(b_last[:, :, 0, :])[..., None] * state
                 + jnp.einsum('bhsk,bhsv->bhkv', ki * jnp.exp(b_last - b), vi))
        return state, o

    state0 = jnp.zeros((bsz, A_HEADS, A_KEY, A_VAL), f32)
    _, o = lax.scan(step, state0, (qc, kc, vc, lfc))
    return o.transpose(1, 0, 3, 2, 4).reshape(bsz, seq, A_HEADS, A_VAL)


def pool_mixer(v, pool_w, pool_scale):
    f32 = jnp.float32
    bsz, seq, _ = v.shape
    vf = v.astype(f32)
    cs = jnp.concatenate([jnp.zeros((bsz, 1, B_WIDTH), f32), lax.cumsum(vf, axis=1)], axis=1)
    pos = jnp.arange(1, seq + 1, dtype=f32)[None, :, None]
    outs = []
    for g, w in enumerate(B_WINDOWS):
        sl = slice(g * B_GW, (g + 1) * B_GW)
        cs_g = cs[:, :, sl]
        shifted = jnp.pad(cs_g, ((0, 0), (w, 0), (0, 0)))[:, :seq + 1]
        mean = (cs_g - shifted)[:, 1:] / jnp.minimum(pos, float(w))
        outs.append(mean - vf[:, :, sl])
    pooled = jnp.stack(outs, axis=2)
    mixed = jnp.einsum('bsgc,gcd->bsgd', pooled, pool_w.astype(f32)).reshape(bsz, seq, B_WIDTH)
    return (mixed * pool_scale.astype(f32)).astype(v.dtype)


def setup_inputs(seed: int = 0) -> dict:
    key = jax.random.key(seed)
    ks = jax.random.split(key, 16)
    nrm = jax.random.normal
    f32 = jnp.float32
    return {
        "x": nrm(ks[0], (BATCH, SEQ, D_MODEL), f32),
        "c": nrm(ks[1], (BATCH, D_MODEL), f32),
        "w_ada": nrm(ks[2], (DEPTH, D_MODEL, 3 * D_MODEL), f32) * (0.5 * D_MODEL ** -0.5),
        "b_ada": nrm(ks[3], (DEPTH, 3 * D_MODEL), f32) * 0.02,
        "norm_pre": 1.0 + 0.1 * nrm(ks[4], (DEPTH, D_MODEL), f32),
        "norm_post": 1.0 + 0.1 * nrm(ks[5], (DEPTH, D_MODEL), f32),
        "w_in": nrm(ks[6], (DEPTH, D_MODEL, IN_COLS), f32) * D_MODEL ** -0.5,
        "lower_bounds": nrm(ks[7], (DEPTH, A_KW), f32),
        "hgrn_norm": 1.0 + 0.1 * nrm(ks[8], (DEPTH, A_VW), f32),
        "pool_w": nrm(ks[9], (DEPTH, B_GROUPS, B_GW, B_GW), f32) * B_GW ** -0.5,
        "pool_scale": 1.0 + 0.1 * nrm(ks[10], (DEPTH, B_WIDTH), f32),
        "w_proj_a": nrm(ks[11], (DEPTH, A_VW, D_MODEL), f32) * A_VW ** -0.5,
        "w_proj_b": nrm(ks[12], (DEPTH, B_WIDTH, D_MODEL), f32) * B_WIDTH ** -0.5,
        "w_out": nrm(ks[13], (DEPTH, D_MODEL, D_MODEL), f32) * D_MODEL ** -0.5,
    }


def reference(x, c, w_ada, b_ada, norm_pre, norm_post, w_in, lower_bounds, hgrn_norm,
              pool_w, pool_scale, w_proj_a, w_proj_b, w_out):
    bsz, seq, _ = x.shape
    sm = jax.nn.softmax(lower_bounds.astype(jnp.float32), axis=0)
    lb_all = jnp.cumsum(sm, axis=0) - sm[0:1]
    c_act = jax.nn.silu(c)
    for l in range(DEPTH):
        mod = c_act @ w_ada[l] + b_ada[l]
        shift, scale, gate = jnp.split(mod, 3, axis=-1)
        h = rmsnorm(x, norm_pre[l]) * (1.0 + scale[:, None, :]) + shift[:, None, :]
        proj = h @ w_in[l]
        q_a, f_a, v_a, z_a, v_b, z_b, g_a, g_b = jnp.split(proj, IN_SPLITS, axis=-1)
        o_a = hgrn2_mixer(q_a, f_a, v_a, lb_all[l])
        o_a = o_a * lax.rsqrt(jnp.mean(o_a * o_a, axis=-1, keepdims=True) + EPS)
        y_a = o_a.reshape(bsz, seq, A_VW).astype(x.dtype) * hgrn_norm[l] * jax.nn.silu(z_a)
        y_b = pool_mixer(v_b, pool_w[l], pool_scale[l]) * jax.nn.silu(z_b)
        merged = jax.nn.sigmoid(g_a) * (y_a @ w_proj_a[l]) + jax.nn.sigmoid(g_b) * (y_b @ w_proj_b[l])
        out = merged @ w_out[l]
        x = x + gate[:, None, :] * rmsnorm(out, norm_post[l])
    return x
```

```python
import numpy as np
from contextlib import ExitStack
import concourse.bass as bass
import concourse.mybir as mybir
from concourse.bass_utils import run_bass_kernel_spmd

F32 = mybir.dt.float32
BF16 = mybir.dt.bfloat16
AF = mybir.ActivationFunctionType
ALU = mybir.AluOpType
AX = mybir.AxisListType

T = 2048
D = 2048
NCORE = 8
EPS = 1e-6
WINS = (2, 4, 8, 16)


class Dep:
    __slots__ = ("w", "r", "dsem", "dcnt")

    def __init__(self):
        self.w = None
        self.r = {}
        self.dsem = None
        self.dcnt = 0


class Eng:
    def __init__(self, name):
        self.name = name
        self.sem = None
        self.cnt = 0
        self.seen = {}
        self.ops = []


class Sched:
    def __init__(self, nc, stack):
        self.nc = nc
        self.stack = stack
        self.sems = []
        self.eng = {n: Eng(n) for n in ("pe", "act", "dve", "pool", "sp")}
        for n in ("pe", "act", "dve", "pool"):
            self.eng[n].sem = self.newsem("e_" + n)
        self.dma_deps = []
        self.named = {}

    def newsem(self, name):
        h = self.stack.enter_context(self.nc.semaphore(name))
        self.sems.append(h)
        return len(self.sems) - 1

    def dep(self, snapshot=False):
        d = Dep()
        if snapshot:
            for n in ("pe", "act", "dve", "pool"):
                e = self.eng[n]
                if e.cnt:
                    d.r[e.sem] = e.cnt
            for o in self.dma_deps:
                if o.dcnt:
                    d.r[o.dsem] = o.dcnt
        return d

    def _waits(self, e, reads, writes):
        need = {}

        def add(tok):
            if tok is not None and need.get(tok[0], 0) < tok[1]:
                need[tok[0]] = tok[1]

        for d in reads:
            add(d.w)
        for d in writes:
            add(d.w)
            for k, v in d.r.items():
                add((k, v))
        for si, v in need.items():
            if e.seen.get(si, 0) < v:
                e.seen[si] = v
                sem = self.sems[si]
                e.ops.append(lambda h, sem=sem, v=v: h.wait_ge(sem, v))

    def _mark(self, tok, reads, writes):
        for d in reads:
            if d.r.get(tok[0], 0) < tok[1]:
                d.r[tok[0]] = tok[1]
        for d in writes:
            d.w = tok
            d.r = {}

    def op(self, en, fn, reads=(), writes=()):
        e = self.eng[en]
        self._waits(e, reads, writes)
        e.cnt += 1
        tok = (e.sem, e.cnt)
        sem = self.sems[e.sem]
        e.ops.append(lambda h, fn=fn, sem=sem: fn(h).then_inc(sem, 1))
        self._mark(tok, reads, writes)

    def group(self, en, fns, reads=(), writes=()):
        e = self.eng[en]
        self._waits(e, reads, writes)
        for fn in fns[:-1]:
            e.ops.append(lambda h, fn=fn: fn(h))
        e.cnt += 1
        tok = (e.sem, e.cnt)
        sem = self.sems[e.sem]
        last = fns[-1]
        e.ops.append(lambda h, fn=last, sem=sem: fn(h).then_inc(sem, 1))
        self._mark(tok, reads, writes)

    def dma(self, en, fn, semdep, reads=(), writes=()):
        e = self.eng[en]
        self._waits(e, reads, writes)
        if semdep.dsem is None:
            semdep.dsem = self.newsem("d%d" % len(self.sems))
            self.dma_deps.append(semdep)
        semdep.dcnt += 16
        tok = (semdep.dsem, semdep.dcnt)
        sem = self.sems[semdep.dsem]
        e.ops.append(lambda h, fn=fn, sem=sem: fn(h).then_inc(sem, 16))
        self._mark(tok, reads, writes)

    def sd(self, name):
        if name not in self.named:
            self.named[name] = Dep()
        return self.named[name]

    def coll(self, en, fn, semdep, reads=(), writes=()):
        e = self.eng[en]
        self._waits(e, reads, writes)
        if semdep.dsem is None:
            semdep.dsem = self.newsem("c%d" % len(self.sems))
            self.dma_deps.append(semdep)
        semdep.dcnt += 1
        tok = (semdep.dsem, semdep.dcnt)
        sem = self.sems[semdep.dsem]
        e.ops.append(lambda h, fn=fn, sem=sem: fn(h).then_inc(sem))
        self._mark(tok, reads, writes)

    def drain(self, en):
        e = self.eng[en]
        for o in self.dma_deps:
            if o.dcnt and e.seen.get(o.dsem, 0) < o.dcnt:
                sem = self.sems[o.dsem]
                e.ops.append(lambda h, sem=sem, v=o.dcnt: h.wait_ge(sem, v))
        for n in ("pe", "act", "dve", "pool"):
            o = self.eng[n]
            if o.cnt:
                sem = self.sems[o.sem]
                e.ops.append(lambda h, sem=sem, v=o.cnt: h.wait_ge(sem, v))


class _Stop(Exception):
    pass


def chk(k):
    import os
    v = int(os.environ.get("KSTOP", "0"))
    if v and k >= v:
        raise _Stop()


def build(NL):
    nc = bass.Bass("TRN2", target_bir_lowering=False)
    dt_in = lambda name, shape: nc.dram_tensor(name, shape, F32, kind="ExternalInput").ap()
    x_in = dt_in("x", [T, D])
    cfm = dt_in("cfm", [128, 16])
    w_ada = dt_in("w_ada", [NL, D, 3 * D])
    w_in = dt_in("w_in", [NL, D, 10240])
    w_pa = dt_in("w_pa", [NL, 1024, D])
    w_pb = dt_in("w_pb", [NL, 1024, D])
    w_out = dt_in("w_out", [NL, D, D])
    pool_w = dt_in("pool_w", [NL, 4, 256, 256])
    bada_fm = dt_in("bada_fm", [4, 128, 32])
    bada_gate = dt_in("bada_gate", [4, D])
    npre_fm = dt_in("npre_fm", [4, 128, 16])
    npost = dt_in("npost", [4, D])
    lb_fm = dt_in("lb_fm", [128, 32])
    hn_fm = dt_in("hn_fm", [4, 128, 8])
    ps_fm = dt_in("ps_fm", [4, 128, 8])
    ident_in = dt_in("ident", [128, 128])
    cmask_in = dt_in("cmask", [128, 128])
    cvec_in = dt_in("cvec", [128, 96])
    out = nc.dram_tensor("out", [T, D], F32, kind="ExternalOutput").ap()
    mg_dram = nc.dram_tensor("mg_scr", [16, 128, T], BF16)
    agin = [[nc.dram_tensor("agin_%d_%d" % (l, h), [128, 130], F32) for h in range(9)] for l in range(NL)]
    agout = [[nc.dram_tensor("agout_%d_%d" % (l, h), [NCORE * 128, 130], F32) for h in range(9)] for l in range(NL)]

    stack = ExitStack()
    with stack:
        arena_t = stack.enter_context(nc.sbuf_tensor("arena", [128, 53000], F32))
        psum = [stack.enter_context(nc.psum_tensor("ps%d" % i, [128, 512], F32)) for i in range(7)]
        ps7 = stack.enter_context(nc.psum_tensor("ps7", [128, 1024], BF16))
        S = Sched(nc, stack)
        PSd = [S.dep() for _ in range(8)]
        PS = [p[:] for p in psum]
        PS7 = ps7[:]

        def f32v(off, n):
            return arena_t[:, off:off + n]

        def bf16v(off, n):
            return arena_t[:, off:off + n // 2].bitcast(BF16)

        O_HT, O_YA, O_YB, O_W0, O_W1, O_WK, O_MISC = 0, 16384, 24576, 32768, 36864, 40960, 47104
        hT = bf16v(O_HT, 16 * T).rearrange("p (a b) -> p a b", b=T)
        YA = bf16v(O_YA, 8 * T).rearrange("p (a b) -> p a b", b=T)
        YB = bf16v(O_YB, 8 * T).rearrange("p (a b) -> p a b", b=T)
        OUTH = f32v(O_YA, 8 * T).rearrange("p (a b) -> p a b", b=T)
        WSL = [bf16v(O_W0, 16 * 512).rearrange("p (a b) -> p a b", b=512),
               bf16v(O_W1, 16 * 512).rearrange("p (a b) -> p a b", b=512)]
        WSd = [S.dep(), S.dep()]

        mo = [O_MISC]

        def misc_f(n):
            o = mo[0]
            mo[0] += n
            assert mo[0] <= 53000
            return f32v(o, n)

        def misc_b(n):
            o = mo[0]
            mo[0] += n // 2
            assert mo[0] <= 53000
            return bf16v(o, n)

        IDENT = misc_b(128)
        ONESB = misc_b(128)
        CMASK = misc_f(128)
        ONESF = misc_f(512)
        CVEC = misc_f(96)
        CA = misc_f(16)
        CAb = misc_b(16)
        LBR = misc_f(32)
        LBS = misc_f(8)
        LBALL = misc_f(32).rearrange("p (h l) -> p h l", l=4)
        OML = misc_f(32).rearrange("p (h l) -> p h l", l=4)
        NOML = misc_f(32).rearrange("p (h l) -> p h l", l=4)
        C2 = misc_f(32).rearrange("p (h l) -> p h l", l=4)
        EPSC = misc_f(1)
        MODFM = misc_f(32)
        BADAF = misc_f(32)
        NPRE = misc_f(16)
        G1 = misc_f(16)
        HN = misc_f(8)
        PSC = misc_f(8)
        POOLW = misc_b(4 * 2 * 256).rearrange("p (g c d) -> p g c d", g=4, c=2)
        GP = misc_f(2048)
        CAREP = misc_b(16 * 128).rearrange("p (a b) -> p a b", b=128)
        cst = S.dep()
        lyr = S.dep()
        GPd = S.dep()
        modd = S.dep()
        POOLWd = S.dep()

        class Work:
            def __init__(self, regions):
                self.regions = [list(r) for r in regions]
                self.i = 0

            def _take(self, nwords):
                nwords = (nwords + 15) // 16 * 16
                while self.i < len(self.regions):
                    r = self.regions[self.i]
                    if r[0] + nwords <= r[1]:
                        o = r[0]
                        r[0] += nwords
                        return o
                    self.i += 1
                raise RuntimeError("work arena overflow")

            def f(self, n):
                return f32v(self._take(n), n)

            def b(self, n):
                return bf16v(self._take(n // 2), n)

        sp, pool = "sp", "pool"
        act = lambda out_, in_, func, reads, writes, **kw: S.op(
            "act", lambda e: e.activation(out=out_, in_=in_, func=func, **kw), reads, writes)
        V = lambda fn, reads, writes: S.op("dve", fn, reads, writes)

        def mm(o, l, r, st, sp_):
            return lambda e: e.matmul(o, l, r, start=st, stop=sp_)

        S.dma(pool, lambda e: e.dma_start(out=IDENT, in_=ident_in), cst, writes=[cst])
        S.dma(sp, lambda e: e.dma_start(out=CMASK, in_=cmask_in), cst, writes=[cst])
        S.dma(sp, lambda e: e.dma_start(out=CVEC, in_=cvec_in), cst, writes=[cst])
        S.dma(sp, lambda e: e.dma_start(out=CA, in_=cfm), cst, writes=[cst])
        S.dma(sp, lambda e: e.dma_start(out=LBR, in_=lb_fm), cst, writes=[cst])
        c2 = S.dep()
        V(lambda e: e.memset(ONESB, 1.0), [], [c2])
        V(lambda e: e.memset(ONESF, 1.0), [], [c2])
        V(lambda e: e.memset(EPSC, EPS), [], [c2])
        LBR3 = LBR.rearrange("p (h l) -> p h l", l=4)
        act(LBR, LBR, AF.Exp, [cst], [c2])
        V(lambda e: e.reduce_sum(out=LBS, in_=LBR3, axis=AX.X), [c2], [c2])
        V(lambda e: e.reciprocal(out=LBS, in_=LBS), [c2], [c2])
        V(lambda e: e.tensor_tensor(out=LBR3, in0=LBR3, in1=LBS.unsqueeze(2).to_broadcast([128, 8, 4]), op=ALU.mult), [c2], [c2])
        V(lambda e: e.memset(LBALL[:, :, 0:1], 0.0), [], [c2])
        V(lambda e: e.tensor_copy(out=LBALL[:, :, 1:2], in_=LBR3[:, :, 1:2]), [c2], [c2])
        V(lambda e: e.tensor_tensor(out=LBALL[:, :, 2:3], in0=LBALL[:, :, 1:2], in1=LBR3[:, :, 2:3], op=ALU.add), [c2], [c2])
        V(lambda e: e.tensor_tensor(out=LBALL[:, :, 3:4], in0=LBALL[:, :, 2:3], in1=LBR3[:, :, 3:4], op=ALU.add), [c2], [c2])
        V(lambda e: e.tensor_scalar(out=OML, in0=LBALL, scalar1=-1.0, scalar2=1.0, op0=ALU.mult, op1=ALU.add), [c2], [c2])
        V(lambda e: e.tensor_scalar(out=NOML, in0=LBALL, scalar1=1.0, scalar2=-1.0, op0=ALU.mult, op1=ALU.add), [c2], [c2])
        V(lambda e: e.tensor_scalar(out=C2, in0=LBALL, scalar1=-1.0, scalar2=1e-30, op0=ALU.mult, op1=ALU.add), [c2], [c2])
        act(CA, CA, AF.Silu, [cst], [c2])
        V(lambda e: e.tensor_copy(out=CAb, in_=CA), [c2], [c2])
        V(lambda e: e.tensor_copy(out=CAREP, in_=CAb.unsqueeze(2).to_broadcast([128, 16, 128])), [c2], [c2])
        CST = [cst, c2]

        hTd = [S.dep() for _ in range(4)]
        YAd = [[S.dep() for _ in range(4)] for _ in range(8)]
        YBd = [[S.dep() for _ in range(4)] for _ in range(8)]
        outd = [S.dep() for _ in range(16)]
        mgd = [S.dep() for _ in range(16)]

        tasks = []

        def wrows(ap2d):
            return ap2d.rearrange("(a p) c -> p a c", p=128)

        for l in range(NL):
            x_src = x_in if l == 0 else out

            def m_load(jb, l=l):
                def f(W, Wd):
                    S.dma(pool, lambda e: e.dma_start(out=W, in_=wrows(w_ada[l, :, jb * 512:(jb + 1) * 512])), Wd, writes=[Wd])
                return f

            def m_comp(jb, l=l):
                def f(W, Wd):
                    if jb == 0:
                        S.dma(sp, lambda e: e.dma_start(out=BADAF, in_=bada_fm[l]), lyr, writes=[lyr, modd])
                        S.dma(sp, lambda e: e.dma_start(out=NPRE, in_=npre_fm[l]), lyr, writes=[lyr])
                        S.dma(sp, lambda e: e.dma_start(out=HN, in_=hn_fm[l]), lyr, writes=[lyr])
                        S.dma(sp, lambda e: e.dma_start(out=PSC, in_=ps_fm[l]), lyr, writes=[lyr])
                        S.dma(pool, lambda e: e.dma_start(out=POOLW, in_=pool_w[l].rearrange("g (c p) d -> p g c d", p=128)), POOLWd, writes=[POOLWd])
                    if jb < 8:
                        fns = []
                        for cc in range(4):
                            col = jb * 4 + cc
                            for dc in range(16):
                                fns.append(mm(PS[0][:, col:col + 1], W[:, dc, cc * 128:(cc + 1) * 128], CAb[:, dc:dc + 1], dc == 0, dc == 15))
                        S.group("pe", fns, [Wd] + CST, [PSd[0]])
                        if jb == 7:
                            V(lambda e: e.tensor_tensor(out=MODFM, in0=PS[0][:, 0:32], in1=BADAF, op=ALU.add), [PSd[0], lyr], [modd])
                            V(lambda e: e.scalar_tensor_tensor(out=G1, in0=MODFM[:, 16:32], scalar=1.0, in1=NPRE, op0=ALU.add, op1=ALU.mult), [modd, lyr], [modd])
                    else:
                        wk = Work([(O_WK, O_MISC)])
                        BR = wk.f(512)
                        NR = wk.f(512)
                        brd = S.dep(snapshot=True)
                        q = jb - 8
                        S.dma(sp, lambda e: e.dma_start(out=BR, in_=bada_gate[l:l + 1, q * 512:(q + 1) * 512].to_broadcast([128, 512])), S.sd("br"), writes=[brd])
                        S.dma(sp, lambda e: e.dma_start(out=NR, in_=npost[l:l + 1, q * 512:(q + 1) * 512].to_broadcast([128, 512])), S.sd("br"), writes=[brd])
                        fns = [mm(PS[1], CAREP[:, dc, :], W[:, dc, :], dc == 0, dc == 15) for dc in range(16)]
                        S.group("pe", fns, [Wd] + CST, [PSd[1]])
                        gsl = GP[:, q * 512:(q + 1) * 512]
                        V(lambda e: e.tensor_tensor(out=BR, in0=PS[1], in1=BR, op=ALU.add), [PSd[1], brd], [brd])
                        V(lambda e: e.tensor_tensor(out=gsl, in0=BR, in1=NR, op=ALU.mult), [brd], [GPd])
                return f

            def p0_comp(l=l, x_src=x_src):
                def f(W, Wd):
                    wk = Work([(O_WK, O_MISC), (O_YB, O_W0)])
                    XT = [wk.f(2048), wk.f(2048)]
                    XN = wk.b(2048)
                    SS = wk.f(16)
                    JK = wk.b(2048)
                    XTd = [S.dep(snapshot=True), S.dep(snapshot=True)]
                    XNd = S.dep(snapshot=True)
                    ssd = S.dep(snapshot=True)
                    for tt in range(16):
                        xt, xd = XT[tt % 2], XTd[tt % 2]
                        tg = tt // 4
                        S.dma(sp, lambda e, xt=xt, tt=tt: e.dma_start(out=xt, in_=x_src[tt * 128:(tt + 1) * 128, :]), S.sd("xt%d" % (tt % 2)),
                              reads=[outd[tt]], writes=[xd])
                        sc = SS[:, tt:tt + 1]
                        V(lambda e, sc=sc: e.memset(sc, 0.0), [], [ssd])
                        act(JK, xt, AF.Square, [xd], [ssd], accum_out=sc)
                        act(sc, sc, AF.Sqrt, [ssd], [ssd], scale=1.0 / D, bias=EPSC)
                        V(lambda e, sc=sc: e.reciprocal(out=sc, in_=sc), [ssd], [ssd])
                        act(XN, xt, AF.Identity, [xd, ssd], [XNd], scale=sc)
                        for half in range(2):
                            fns = [lambda e, j=j, half=half: e.transpose(PS7[:, j * 128:(j + 1) * 128], XN[:, (half * 8 + j) * 128:(half * 8 + j + 1) * 128], IDENT) for j in range(8)]
                            S.group("pe", fns, [XNd] + CST, [PSd[7]])
                            for j in range(8):
                                dc = half * 8 + j
                                act(hT[:, dc, tt * 128:(tt + 1) * 128], PS7[:, j * 128:(j + 1) * 128], AF.Identity,
                                    [PSd[7], modd], [hTd[tg]], scale=G1[:, dc:dc + 1], bias=MODFM[:, dc:dc + 1])
                return f

            def a_load(h, l=l):
                def f(W, Wd):
                    for i, base in enumerate((0, 1024, 2048, 3072)):
                        c0 = base + h * 128
                        S.dma(pool, lambda e, i=i, c0=c0: e.dma_start(out=W[:, :, i * 128:(i + 1) * 128], in_=wrows(w_in[l, :, c0:c0 + 128])), Wd, writes=[Wd])
                return f

            astate = {}

            def a_comp(h, l=l, astate=astate):
                def f(W, Wd):
                    nd = lambda: S.dep(snapshot=True)
                    st = astate
                    if h == 0:
                        wk = Work([(O_YB, O_W0), (O_WK, O_MISC)])
                        st.clear()
                        st["OL"] = wk.f(2048); st["OLd"] = [nd() for _ in range(4)]
                        st["QH"] = [wk.b(2048), wk.b(2048)]; st["QHd"] = [[nd() for _ in range(4)] for _ in range(2)]
                        st["SZ"] = wk.b(2048); st["SZd"] = [nd() for _ in range(4)]
                        st["F1"] = wk.f(512); st["F1d"] = nd()
                        st["F2"] = wk.f(512); st["F2d"] = nd()
                        for nm in ("SQ", "F3", "F4", "F5", "F6"):
                            st[nm] = wk.b(512); st[nm + "d"] = nd()
                        for nm in ("QT", "KT", "VT"):
                            st[nm] = [wk.b(512), wk.b(512)]; st[nm + "d"] = [nd(), nd()]
                        st["BM"] = [wk.f(32), wk.f(32)]; st["BMd"] = [nd(), nd()]
                        st["EE"] = [wk.f(32), wk.f(32)]; st["EEd"] = [nd(), nd()]
                        st["KTTf"] = [wk.b(512) for _ in range(4)]; st["KTTd"] = nd()
                        st["SC"] = wk.b(512).rearrange("p (a b) -> p a b", b=128); st["SCd"] = nd()
                        st["ST"] = wk.f(128); st["STd"] = [nd(), nd()]
                        st["ST2"] = wk.f(128)
                        st["KVS"] = wk.f(512); st["KVSd"] = nd()
                        st["SB"] = wk.b(16 * 128).rearrange("p (a b) -> p a b", b=128); st["SBd"] = nd()
                        st["AGS"] = wk.f(130); st["AGSd"] = nd()
                        st["AGR"] = wk.f(8 * 130).rearrange("p (a b) -> p a b", b=130); st["AGRd"] = nd()
                        st["ACC"] = wk.f(128); st["T1"] = wk.f(128); st["DP"] = wk.f(8); st["ACCd"] = nd()
                        st["SINB"] = wk.b(128); st["SINd"] = nd()
                        st["SQO"] = st["F3"]; st["SQOd"] = st["F3d"]
                        st["RS"] = st["F2"]; st["RSd"] = st["F2d"]
                    OL, OLd, SZ, SZd = st["OL"], st["OLd"], st["SZ"], st["SZd"]
                    F1, F1d, F2, F2d = st["F1"], st["F1d"], st["F2"], st["F2d"]
                    SQ, SQd, F3, F3d, F4, F4d, F5, F5d, F6, F6d = (st[k] for k in ("SQ", "SQd", "F3", "F3d", "F4", "F4d", "F5", "F5d", "F6", "F6d"))
                    KTTf, KTTd, SC, SCd, ST, STd, SB, SBd = (st[k] for k in ("KTTf", "KTTd", "SC", "SCd", "ST", "STd", "SB", "SBd"))
                    KTT = [k_.rearrange("p (a b) -> p a b", b=128) for k_ in KTTf]
                    AGS, AGSd, AGR, AGRd, ACC, T1, DP, ACCd = (st[k] for k in ("AGS", "AGSd", "AGR", "AGRd", "ACC", "T1", "DP", "ACCd"))
                    SINB, SINd, SQO, SQOd, RS, RSd = (st[k] for k in ("SINB", "SINd", "SQO", "SQOd", "RS", "RSd"))
                    hp = h % 2
                    lbs = LBALL[:, h, l:l + 1]; oml = OML[:, h, l:l + 1]; noml = NOML[:, h, l:l + 1]; c2s = C2[:, h, l:l + 1]
                    PL = lambda fn, reads, writes: S.op("pool", fn, reads, writes)

                    ST2, KVS, KVSd = st["ST2"], st["KVS"], st["KVSd"]

                    def G(tg):
                        tok = slice(tg * 512, (tg + 1) * 512)
                        steps = []
                        for (bank, c0) in ((0, 0), (1, 128)):
                            steps.append(lambda bank=bank, c0=c0: S.group(
                                "pe", [mm(PS[bank], W[:, dc, c0:c0 + 128], hT[:, dc, tok], dc == 0, dc == 15) for dc in range(16)],
                                [Wd, hTd[tg]], [PSd[bank]]))

                        def gv():
                            fns = []
                            for ti in range(4):
                                t0 = tg * 512 + ti * 128
                                for dc in range(16):
                                    fns.append(mm(PS[3][:, ti * 128:(ti + 1) * 128], hT[:, dc, t0:t0 + 128], W[:, dc, 256:384], dc == 0, dc == 15))
                            S.group("pe", fns, [Wd, hTd[tg]], [PSd[3]])
                        steps.append(gv)
                        return steps

                    def Erest(tg):
                        par = tg % 2
                        tok = slice(tg * 512, (tg + 1) * 512)
                        QT, QTd, KT, KTd = st["QT"][par], st["QTd"][par], st["KT"][par], st["KTd"][par]
                        VTf, VTd = st["VT"][par], st["VTd"][par]
                        BM, BMd, EE, EEd = st["BM"][par], st["BMd"][par], st["EE"][par], st["EEd"][par]
                        QH, QHd = st["QH"][hp], st["QHd"][hp]
                        steps = []
                        ap = steps.append
                        ap(lambda: act(F1, PS[1], AF.Sigmoid, [PSd[1]], [F1d]))
                        ap(lambda: act(SQ, PS[0], AF.Silu, [PSd[0]], [SQd]))
                        ap(lambda: act(VTf, PS[3], AF.Copy, [PSd[3]], [VTd]))
                        ap(lambda: PL(lambda e: e.tensor_scalar(out=F5, in0=F1, scalar1=noml, scalar2=oml, op0=ALU.mult, op1=ALU.add), [F1d] + CST, [F5d]))
                        ap(lambda: V(lambda e: e.tensor_scalar(out=F1, in0=F1, scalar1=oml, scalar2=c2s, op0=ALU.mult, op1=ALU.max), CST, [F1d]))
                        ap(lambda: act(F1, F1, AF.Ln, CST, [F1d], bias=lbs, scale=1.0))
                        if tg > 0:
                            BMp, BMpd = st["BM"][1 - par], st["BMd"][1 - par]
                            ap(lambda: V(lambda e: e.tensor_copy(out=BM[:, 0:1], in_=BMp[:, 17:18]), [BMpd], [BMd]))
                        else:
                            ap(lambda: V(lambda e: e.memset(BM[:, 0:1], 0.0), [], [BMd]))
                        ap(lambda: V(lambda e: e.tensor_tensor_scan(out=F2, data0=ONESF, data1=F1, initial=BM[:, 0:1], op0=ALU.mult, op1=ALU.add),
                                     [F1d, BMd] + CST, [F2d]))
                        ap(lambda: V(lambda e: e.tensor_copy(out=BM[:, 1:17], in_=F2[:, 15:512:32]), [F2d], [BMd]))
                        ap(lambda: V(lambda e: e.tensor_copy(out=BM[:, 17:18], in_=F2[:, 511:512]), [F2d], [BMd]))
                        ap(lambda: V(lambda e: e.tensor_tensor(out=EE[:, 0:17], in0=BM[:, 1:18], in1=BM[:, 0:17], op=ALU.subtract), [BMd], [EEd]))
                        ap(lambda: act(EE[:, 0:17], EE[:, 0:17], AF.Exp, [], [EEd]))
                        ap(lambda: V(lambda e: e.tensor_tensor(out=F1.rearrange("p (a b) -> p a b", b=32), in0=F2.rearrange("p (a b) -> p a b", b=32),
                                                               in1=BM[:, 1:17].unsqueeze(2).to_broadcast([128, 16, 32]), op=ALU.subtract), [F2d, BMd], [F1d]))
                        ap(lambda: V(lambda e: e.tensor_scalar(out=F1, in0=F1, scalar1=40.0, scalar2=-40.0, op0=ALU.min, op1=ALU.max), [], [F1d]))
                        ap(lambda: act(F3, F1, AF.Exp, [F1d], [F3d]))
                        ap(lambda: act(F4, F1, AF.Exp, [F1d], [F4d], scale=-1.0))
                        ap(lambda: act(F6, F2, AF.Exp, [F2d], [F6d]))
                        ap(lambda: PL(lambda e: e.tensor_tensor(out=KT, in0=F5, in1=F4, op=ALU.mult), [F5d, F4d], [KTd]))
                        ap(lambda: PL(lambda e: e.tensor_tensor(out=QT, in0=SQ, in1=F3, op=ALU.mult), [SQd, F3d], [QTd]))
                        ap(lambda: PL(lambda e: e.tensor_tensor(out=QH[:, tok], in0=SQ, in1=F6, op=ALU.mult), [SQd, F6d], [QHd[tg]]))
                        return steps

                    def L(tg):
                        par = tg % 2
                        tok = slice(tg * 512, (tg + 1) * 512)
                        QT, QTd, KT, KTd = st["QT"][par], st["QTd"][par], st["KT"][par], st["KTd"][par]
                        VTf, VTd = st["VT"][par], st["VTd"][par]
                        VT = VTf.rearrange("p (a b) -> p a b", b=128)
                        BM, BMd, EE, EEd = st["BM"][par], st["BMd"][par], st["EE"][par], st["EEd"][par]
                        steps = []
                        ap = steps.append
                        STS = [ST, ST2]
                        if tg == 0:
                            ap(lambda: V(lambda e: e.memset(ST, 0.0), [], [STd[0]]))
                        ap(lambda: S.group("pe", [lambda e, ti=ti: e.transpose(PS7[:, ti * 128:(ti + 1) * 128], KT[:, ti * 128:(ti + 1) * 128], IDENT) for ti in range(4)],
                                           [KTd] + CST, [PSd[7]]))
                        for r4 in range(4):
                            ap(lambda r4=r4: act(KTTf[r4], PS7[:, 0:512], AF.Identity, [PSd[7]] + CST, [KTTd], scale=CVEC[:, 80 + r4:81 + r4]))
                        ap(lambda: S.group("pe", [mm(PS[4][:, ti * 128:(ti + 1) * 128], KT[:, ti * 128:(ti + 1) * 128], QT[:, ti * 128:(ti + 1) * 128], True, True) for ti in range(4)],
                                           [KTd, QTd], [PSd[4]]))
                        ap(lambda: V(lambda e: e.tensor_tensor(out=SC, in0=PS[4].rearrange("p (a b) -> p a b", b=128),
                                                               in1=CMASK.unsqueeze(1).to_broadcast([128, 4, 128]), op=ALU.mult), [PSd[4]] + CST, [SCd]))
                        ap(lambda: V(lambda e: e.tensor_scalar(out=ST, in0=ST, scalar1=EE[:, 0:1], scalar2=None, op0=ALU.mult), [EEd], [STd[0]]))
                        ap(lambda: PL(lambda e: e.tensor_copy(out=SB[:, 0, :], in_=ST), [STd[0]], [SBd]))
                        for rnd in range(4):
                            bank = 5 + rnd % 2

                            def kvr(rnd=rnd, bank=bank):
                                fns = []
                                for cc in range(4):
                                    c = rnd * 4 + cc
                                    fns.append(mm(PS[bank][:, cc * 128:(cc + 1) * 128], KTT[c % 4][:, c // 4, :], VT[:, c // 4, :], True, True))
                                S.group("pe", fns, [KTTd, VTd], [PSd[bank]])
                            ap(kvr)
                            for cc in range(4):
                                c = rnd * 4 + cc
                                ap(lambda bank=bank, cc=cc, c=c: act(KVS[:, cc * 128:(cc + 1) * 128], PS[bank][:, cc * 128:(cc + 1) * 128], AF.Identity,
                                                                     [PSd[bank], EEd], [KVSd], scale=EE[:, c + 1:c + 2]))
                            for cc in range(4):
                                c = rnd * 4 + cc
                                so, sn = STS[c % 2], STS[(c + 1) % 2]
                                ap(lambda c=c, cc=cc, so=so, sn=sn: V(lambda e: e.scalar_tensor_tensor(out=sn, in0=so, scalar=EE[:, c + 1:c + 2], in1=KVS[:, cc * 128:(cc + 1) * 128],
                                                                                                    op0=ALU.mult, op1=ALU.add), [KVSd, EEd, STd[c % 2]], [STd[(c + 1) % 2]]))
                                if c < 15:
                                    ap(lambda c=c, sn=sn: PL(lambda e: e.tensor_copy(out=SB[:, c + 1, :], in_=sn), [STd[(c + 1) % 2]], [SBd]))

                        def omm():
                            fns = []
                            for c in range(16):
                                ti, r0 = c // 4, (c % 4) * 32
                                cols = slice(c * 32, (c + 1) * 32)
                                fns.append(mm(PS[2][:, cols], VT[:, ti, :], SC[:, ti, r0:r0 + 32], True, False))
                                fns.append(mm(PS[2][:, cols], SB[:, c, :], QT[:, cols], False, True))
                            S.group("pe", fns, [VTd, SCd, SBd, QTd], [PSd[2]])
                        ap(omm)
                        ap(lambda: act(OL[:, tok], PS[2], AF.Copy, [PSd[2]], [OLd[tg]]))
                        if tg == 3:
                            def agx():
                                V(lambda e: e.tensor_copy(out=AGS[:, 0:128], in_=ST), [STd[0]], [AGSd])
                                act(AGS[:, 128:130], BM[:, 17:18].to_broadcast([128, 2]), AF.Exp, [BMd], [AGSd])
                                gi, go = agin[l][h], agout[l][h]
                                gid, god = S.dep(), S.dep()
                                S.dma(sp, lambda e: e.dma_start(out=gi[:, :], in_=AGS), S.sd("agi"), reads=[AGSd], writes=[gid])
                                S.coll(pool, lambda e: e.collective_compute("AllGather", ALU.bypass, replica_groups=[list(range(NCORE))],
                                                                            ins=[gi.ap().opt()], outs=[go.ap().opt()]), S.sd("ago"), reads=[gid], writes=[god])
                                S.dma(sp, lambda e: e.dma_start(out=AGR, in_=go.ap().rearrange("(r p) f -> p r f", p=128)), S.sd("agr"), reads=[god], writes=[AGRd])
                            ap(agx)
                            for tz in range(4):
                                tkz = slice(tz * 512, (tz + 1) * 512)
                                ap(lambda tz=tz, tkz=tkz: S.group("pe", [mm(PS[3], W[:, dc, 384:512], hT[:, dc, tkz], dc == 0, dc == 15) for dc in range(16)],
                                                                  [Wd, hTd[tz]], [PSd[3]]))
                                ap(lambda tz=tz, tkz=tkz: act(SZ[:, tkz], PS[3], AF.Silu, [PSd[3]], [SZd[tz]]))
                        return steps

                    def merge(Ls, Bs=(), Cs=(), cpos=(14, 28, 42)):
                        bi = ci = 0
                        n = len(Ls)
                        for i, a_ in enumerate(Ls):
                            a_()
                            while bi < len(Bs) and (bi + 1) * n <= (i + 1) * len(Bs):
                                Bs[bi]()
                                bi += 1
                            while ci < len(Cs) and i >= cpos[min(ci, len(cpos) - 1)]:
                                Cs[ci]()
                                ci += 1
                        for k in range(bi, len(Bs)):
                            Bs[k]()
                        for k in range(ci, len(Cs)):
                            Cs[k]()

                    def P(hh):
                        QH, QHd = st["QH"][hh % 2], st["QHd"][hh % 2]
                        V(lambda e: e.memset(ACC, 0.0), [], [ACCd])
                        for j in range(8):
                            mj = CVEC[:, j:j + 1]
                            V(lambda e, j=j, mj=mj: e.tensor_scalar(out=DP[:, j:j + 1], in0=AGR[:, j, 128:129], scalar1=-1.0, scalar2=mj, op0=ALU.add, op1=ALU.mult),
                              [AGRd] + CST, [ACCd])
                            V(lambda e, j=j: e.tensor_scalar(out=T1, in0=ACC, scalar1=DP[:, j:j + 1], scalar2=None, op0=ALU.mult), [], [ACCd])
                            V(lambda e: e.tensor_tensor(out=ACC, in0=ACC, in1=T1, op=ALU.add), [], [ACCd])
                            V(lambda e, j=j, mj=mj: e.scalar_tensor_tensor(out=ACC, in0=AGR[:, j, 0:128], scalar=mj, in1=ACC, op0=ALU.mult, op1=ALU.add), [AGRd], [ACCd])
                        V(lambda e: e.tensor_copy(out=SINB, in_=ACC), [ACCd], [SINd])
                        for tg in range(4):
                            tok = slice(tg * 512, (tg + 1) * 512)
                            S.group("pe", [mm(PS[4], SINB, QH[:, tok], True, True)], [SINd, QHd[tg]], [PSd[4]])
                            V(lambda e, tok=tok: e.tensor_tensor(out=OL[:, tok], in0=PS[4], in1=OL[:, tok], op=ALU.add), [PSd[4]], [OLd[tg]])
                            act(SQO, OL[:, tok], AF.Square, [OLd[tg]], [SQOd])
                            S.group("pe", [mm(PS[5], ONESB, SQO, True, True)], [SQOd] + CST, [PSd[5]])
                            act(RS, PS[5], AF.Sqrt, [PSd[5]] + CST, [RSd], scale=1.0 / 128, bias=EPSC)
                            V(lambda e: e.reciprocal(out=RS, in_=RS), [], [RSd])
                            V(lambda e, tok=tok: e.tensor_tensor(out=OL[:, tok], in0=OL[:, tok], in1=RS, op=ALU.mult), [RSd], [OLd[tg]])
                            V(lambda e, tok=tok, hh=hh: e.scalar_tensor_tensor(out=YA[:, hh, tok], in0=OL[:, tok], scalar=HN[:, hh:hh + 1], in1=SZ[:, tok], op0=ALU.mult, op1=ALU.mult),
                              [OLd[tg], SZd[tg], lyr], [YAd[hh][tg]])

                    merge(G(0))
                    merge(Erest(0), Cs=G(1), cpos=(2, 8, 14))
                    if h > 0:
                        P(h - 1)
                    merge(L(0), Erest(1), G(2))
                    merge(L(1), Erest(2), G(3))
                    merge(L(2), Erest(3))
                    merge(L(3))
                    if h == 7:
                        P(7)
                return f

            bstate = {}

            def b_load(g, l=l):
                def f(W, Wd):
                    S.dma(pool, lambda e: e.dma_start(out=W[:, :, 0:256], in_=wrows(w_in[l, :, 4096 + g * 256:4096 + (g + 1) * 256])), Wd, writes=[Wd])
                    S.dma(pool, lambda e: e.dma_start(out=W[:, :, 256:512], in_=wrows(w_in[l, :, 5120 + g * 256:5120 + (g + 1) * 256])), Wd, writes=[Wd])
                return f

            def b_comp(g, l=l, bstate=bstate):
                def f(W, Wd):
                    nd = lambda: S.dep(snapshot=True)
                    if g == 0:
                        wk = Work([(O_WK, O_MISC)])
                        bstate["wk"] = wk
                        bstate["FIRST"] = wk.f(8 * 32).rearrange("p (a b) -> p a b", b=32)
                        bstate["HAL"] = wk.f(130)
                        bstate["SZF"] = wk.f(8 * 16).rearrange("p (a b) -> p a b", b=16)
                        bstate["VBE"] = [wk.f(528), wk.f(528)]
                        bstate["TA"] = wk.f(528)
                        bstate["TB"] = wk.f(528)
                        bstate["PBT"] = [wk.b(512), wk.b(512)]
                        for k in ("FIRSTd", "HALd", "SZFd", "TAd", "PBTd0", "PBTd1", "VBEd0", "VBEd1"):
                            bstate[k] = nd()
                    wk = bstate["wk"]
                    FIRST, HAL, SZF, VBE, TA, TB, PBT = (bstate[k] for k in ("FIRST", "HAL", "SZF", "VBE", "TA", "TB", "PBT"))
                    FIRSTd, HALd, SZFd, TAd = (bstate[k] for k in ("FIRSTd", "HALd", "SZFd", "TAd"))
                    PBTd = [bstate["PBTd0"], bstate["PBTd1"]]
                    VBEd = [bstate["VBEd0"], bstate["VBEd1"]]
                    w = WINS[g]
                    for q in range(2):
                        V(lambda e, q=q: e.memset(VBE[q][:, 0:16], 0.0), [], [VBEd[q]])
                    for tg in range(4):
                        tok = slice(tg * 512, (tg + 1) * 512)
                        for q in range(4):
                            S.group("pe", [mm(PS[q], W[:, dc, q * 128:(q + 1) * 128], hT[:, dc, tok], dc == 0, dc == 15) for dc in range(16)],
                                    [Wd, hTd[tg]], [PSd[q]])
                        for q in range(2):
                            j = g * 2 + q
                            act(VBE[q][:, 16:528], PS[q], AF.Copy, [PSd[q]], [VBEd[q]])
                            act(YB[:, j, tok], PS[2 + q], AF.Silu, [PSd[2 + q]], [YBd[j][tg]])
                            if tg == 0:
                                V(lambda e, q=q, j=j: e.tensor_copy(out=FIRST[:, j, 16:32], in_=VBE[q][:, 16:32]), [VBEd[q]], [FIRSTd])
                                V(lambda e, j=j: e.tensor_copy(out=SZF[:, j, :], in_=YB[:, j, 0:16]), [YBd[j][0]], [SZFd])
                            src = VBE[q]
                            sh = 1
                            lo = 0
                            bufs = [TA, TB]
                            bi = 0
                            while sh < w:
                                dst = bufs[bi]
                                lo += sh
                                V(lambda e, src=src, dst=dst, sh=sh, lo=lo: e.tensor_tensor(out=dst[:, lo:528], in0=src[:, lo:528], in1=src[:, lo - sh:528 - sh], op=ALU.add),
                                  [VBEd[q]], [TAd])
                                src = dst
                                bi ^= 1
                                sh *= 2
                            V(lambda e, src=src, q=q: e.scalar_tensor_tensor(out=PBT[q], in0=src[:, 16:528], scalar=1.0 / w, in1=VBE[q][:, 16:528], op0=ALU.mult, op1=ALU.subtract),
                              [TAd, VBEd[q]], [PBTd[q]])
                            if tg == 3:
                                V(lambda e, q=q, j=j: e.tensor_copy(out=HAL[:, j * 16:(j + 1) * 16], in_=VBE[q][:, 512:528]), [VBEd[q]], [HALd])
                            else:
                                V(lambda e, q=q: e.tensor_copy(out=VBE[q][:, 0:16], in_=VBE[q][:, 512:528]), [TAd, PBTd[q]], [VBEd[q]])
                        for q in range(2):
                            j = g * 2 + q
                            S.group("pe", [mm(PS[4 + q], POOLW[:, g, cq, q * 128:(q + 1) * 128], PBT[cq], cq == 0, cq == 1) for cq in range(2)],
                                    [PBTd[0], PBTd[1], POOLWd], [PSd[4 + q]])
                            V(lambda e, q=q, j=j, tok=tok: e.scalar_tensor_tensor(out=YB[:, j, tok], in0=PS[4 + q], scalar=PSC[:, j:j + 1], in1=YB[:, j, tok], op0=ALU.mult, op1=ALU.mult),
                              [PSd[4 + q], lyr], [YBd[j][tg]])
                    if g == 3:
                        AGR = wk.f(8 * 130).rearrange("p (a b) -> p a b", b=130)
                        AGRd = nd()
                        PF = wk.b(8 * 16).rearrange("p (a b) -> p a b", b=16)
                        PFd = nd()
                        FA = wk.f(8 * 32).rearrange("p (a b) -> p a b", b=32)
                        FB = wk.f(8 * 32).rearrange("p (a b) -> p a b", b=32)
                        FAd = nd()
                        gi, go = agin[l][8], agout[l][8]
                        gid, god = S.dep(), S.dep()
                        V(lambda e: e.memset(HAL[:, 128:130], 0.0), [], [HALd])
                        S.dma(sp, lambda e: e.dma_start(out=gi[:, :], in_=HAL), S.sd("agi"), reads=[HALd], writes=[gid])
                        S.coll(pool, lambda e: e.collective_compute("AllGather", ALU.bypass, replica_groups=[list(range(NCORE))],
                                                                    ins=[gi.ap().opt()], outs=[go.ap().opt()]), S.sd("ago"), reads=[gid], writes=[god])
                        S.dma(sp, lambda e: e.dma_start(out=AGR, in_=go.ap().rearrange("(r p) f -> p r f", p=128)), S.sd("agr"), reads=[god], writes=[AGRd])
                        V(lambda e: e.memset(FIRST[:, :, 0:16], 0.0), [], [FIRSTd])
                        for r in range(8):
                            V(lambda e, r=r: e.scalar_tensor_tensor(out=FIRST[:, :, 0:16], in0=AGR[:, r, 0:128].rearrange("p (a b) -> p a b", b=16),
                                                                    scalar=CVEC[:, 8 + r:9 + r], in1=FIRST[:, :, 0:16], op0=ALU.mult, op1=ALU.add),
                              [AGRd] + CST, [FIRSTd])
                        for gg in range(4):
                            ww = WINS[gg]
                            js = slice(gg * 2, gg * 2 + 2)
                            src = FIRST
                            sh = 1
                            lo = 0
                            bufs = [FA, FB]
                            bi = 0
                            while sh < ww:
                                dst = bufs[bi]
                                lo += sh
                                V(lambda e, src=src, dst=dst, sh=sh, js=js, lo=lo: e.tensor_tensor(out=dst[:, js, lo:32], in0=src[:, js, lo:32], in1=src[:, js, lo - sh:32 - sh], op=ALU.add),
                                  [FIRSTd], [FAd])
                                src = dst
                                bi ^= 1
                                sh *= 2
                            rd = CVEC[:, 16 + gg * 16:32 + gg * 16]
                            V(lambda e, src=src, js=js, rd=rd: e.tensor_tensor(out=src[:, js, 16:32], in0=src[:, js, 16:32], in1=rd.unsqueeze(1).to_broadcast([128, 2, 16]), op=ALU.mult),
                              CST, [FAd])
                            V(lambda e, src=src, js=js: e.tensor_tensor(out=PF[:, js, :], in0=src[:, js, 16:32], in1=FIRST[:, js, 16:32], op=ALU.subtract), [FAd, FIRSTd], [PFd])
                        for gg in range(4):
                            for q in range(2):
                                j = gg * 2 + q
                                S.group("pe", [mm(PS[4 + q][:, 0:16], POOLW[:, gg, cq, q * 128:(q + 1) * 128], PF[:, gg * 2 + cq, :], cq == 0, cq == 1) for cq in range(2)],
                                        [PFd, POOLWd], [PSd[4 + q]])
                                V(lambda e, q=q, j=j: e.scalar_tensor_tensor(out=YB[:, j, 0:16], in0=PS[4 + q][:, 0:16], scalar=PSC[:, j:j + 1], in1=SZF[:, j, :], op0=ALU.mult, op1=ALU.mult),
                                  [PSd[4 + q], SZFd, lyr], [YBd[j][0]])
                return f

            cstate = {}

            def c_load(dco, l=l):
                def f(W, Wd):
                    c0 = dco * 128
                    S.dma(pool, lambda e: e.dma_start(out=W[:, :, 0:128], in_=wrows(w_in[l, :, 6144 + c0:6144 + c0 + 128])), Wd, writes=[Wd])
                    S.dma(pool, lambda e: e.dma_start(out=W[:, :, 128:256], in_=wrows(w_in[l, :, 8192 + c0:8192 + c0 + 128])), Wd, writes=[Wd])
                    S.dma(pool, lambda e: e.dma_start(out=W[:, 0:8, 256:384], in_=wrows(w_pa[l, :, c0:c0 + 128])), Wd, writes=[Wd])
                    S.dma(pool, lambda e: e.dma_start(out=W[:, 0:8, 384:512], in_=wrows(w_pb[l, :, c0:c0 + 128])), Wd, writes=[Wd])
                return f

            def c_comp(dco, l=l, cstate=cstate):
                def f(W, Wd):
                    nd = lambda: S.dep(snapshot=True)
                    if dco == 0:
                        wk = Work([(O_WK, O_MISC)])
                        cstate["MG"] = [wk.b(2048), wk.b(2048)]
                        cstate["MGd"] = [nd(), nd()]
                        cstate["SA"] = wk.f(512); cstate["SB_"] = wk.f(512)
                        cstate["SAd"] = nd(); cstate["SBd"] = nd()
                    MG, MGd = cstate["MG"][dco % 2], cstate["MGd"][dco % 2]
                    SA, SBb, SAd, SBd2 = cstate["SA"], cstate["SB_"], cstate["SAd"], cstate["SBd"]
                    for tg in range(4):
                        tok = slice(tg * 512, (tg + 1) * 512)
                        S.group("pe", [mm(PS[0], W[:, dc, 0:128], hT[:, dc, tok], dc == 0, dc == 15) for dc in range(16)], [Wd, hTd[tg]], [PSd[0]])
                        S.group("pe", [mm(PS[1], W[:, dc, 128:256], hT[:, dc, tok], dc == 0, dc == 15) for dc in range(16)], [Wd, hTd[tg]], [PSd[1]])
                        S.group("pe", [mm(PS[2], W[:, kc, 256:384], YA[:, kc, tok], kc == 0, kc == 7) for kc in range(8)],
                                [Wd] + [YAd[kc][tg] for kc in range(8)], [PSd[2]])
                        S.group("pe", [mm(PS[3], W[:, kc, 384:512], YB[:, kc, tok], kc == 0, kc == 7) for kc in range(8)],
                                [Wd] + [YBd[kc][tg] for kc in range(8)], [PSd[3]])
                        act(SA, PS[0], AF.Sigmoid, [PSd[0]], [SAd])
                        act(SBb, PS[1], AF.Sigmoid, [PSd[1]], [SBd2])
                        V(lambda e: e.tensor_tensor(out=SA, in0=PS[2], in1=SA, op=ALU.mult), [PSd[2]], [SAd])
                        V(lambda e: e.tensor_tensor(out=SBb, in0=PS[3], in1=SBb, op=ALU.mult), [PSd[3]], [SBd2])
                        V(lambda e, tok=tok, MG=MG: e.tensor_tensor(out=MG[:, tok], in0=SA, in1=SBb, op=ALU.add), [SAd, SBd2], [MGd])
                    S.dma(sp, lambda e, MG=MG: e.dma_start(out=mg_dram[dco], in_=MG), S.sd("mg%d" % (dco % 2)), reads=[MGd], writes=[mgd[dco]])
                return f

            dstate = {}

            def d_load(hf, cb, l=l):
                def f(W, Wd):
                    S.dma(pool, lambda e: e.dma_start(out=W, in_=wrows(w_out[l, :, cb * 512:(cb + 1) * 512])), Wd, writes=[Wd])
                return f

            def d_comp(hf, cb, l=l, dstate=dstate, x_src=x_src):
                def f(W, Wd):
                    nd = lambda: S.dep(snapshot=True)
                    if hf == 0 and cb == 0:
                        for dc in range(16):
                            S.dma(sp, lambda e, dc=dc: e.dma_start(out=hT[:, dc, :], in_=mg_dram[dc]), S.sd("mgld"),
                                  reads=[mgd[dc]], writes=hTd)
                        wk = Work([(O_WK, O_MISC)])
                        dstate["XT"] = [wk.f(2048), wk.f(2048)]
                        dstate["XTd"] = [nd(), nd()]
                        dstate["JK"] = wk.b(512); dstate["JKd"] = nd()
                        dstate["SS"] = wk.f(64).rearrange("p (a b) -> p a b", b=4); dstate["SSd"] = nd()
                        dstate["RS"] = wk.f(16)
                        dstate["OUTd"] = [nd() for _ in range(8)]
                    XT, XTd, JK, JKd, SS, SSd, RSs, OUTd = (dstate[k] for k in ("XT", "XTd", "JK", "JKd", "SS", "SSd", "RS", "OUTd"))
                    for t8 in range(8):
                        tt = hf * 8 + t8
                        S.group("pe", [mm(PS[t8 % 4], hT[:, dc, tt * 128:(tt + 1) * 128], W[:, dc, :], dc == 0, dc == 15) for dc in range(16)],
                                [Wd] + hTd, [PSd[t8 % 4]])
                        act(OUTH[:, t8, cb * 512:(cb + 1) * 512], PS[t8 % 4], AF.Copy, [PSd[t8 % 4]], [OUTd[t8]])
                        V(lambda e, tt=tt: e.memset(SS[:, tt, cb:cb + 1], 0.0), [], [SSd])
                        act(JK, PS[t8 % 4], AF.Square, [PSd[t8 % 4]], [JKd, SSd], accum_out=SS[:, tt, cb:cb + 1])
                        if cb == 3:
                            xt, xd = XT[tt % 2], XTd[tt % 2]
                            S.dma(sp, lambda e, xt=xt, tt=tt: e.dma_start(out=xt, in_=x_src[tt * 128:(tt + 1) * 128, :]), S.sd("xt%d" % (tt % 2)), reads=[outd[tt]], writes=[xd])
                            rs = RSs[:, tt:tt + 1]
                            V(lambda e, tt=tt, rs=rs: e.reduce_sum(out=rs, in_=SS[:, tt, :], axis=AX.X), [SSd], [SSd])
                            act(rs, rs, AF.Sqrt, CST, [SSd], scale=1.0 / D, bias=EPSC)
                            V(lambda e, rs=rs: e.reciprocal(out=rs, in_=rs), [], [SSd])
                            V(lambda e, t8=t8, rs=rs: e.scalar_tensor_tensor(out=OUTH[:, t8, :], in0=OUTH[:, t8, :], scalar=rs, in1=GP, op0=ALU.mult, op1=ALU.mult),
                              [SSd, GPd], [OUTd[t8]])
                            V(lambda e, t8=t8, xt=xt: e.tensor_tensor(out=xt, in0=OUTH[:, t8, :], in1=xt, op=ALU.add), [OUTd[t8]], [xd])
                            S.dma(sp, lambda e, xt=xt, tt=tt: e.dma_start(out=out[tt * 128:(tt + 1) * 128, :], in_=xt), S.sd("xt%d" % (tt % 2)), reads=[xd], writes=[outd[tt]])
                return f

            for jb in range(8):
                tasks.append((m_load(jb), m_comp(jb)))
            tasks.append((None, p0_comp()))
            for jb in range(8, 12):
                tasks.append((m_load(jb), m_comp(jb)))
            for h in range(8):
                tasks.append((a_load(h), a_comp(h)))
            for g in range(4):
                tasks.append((b_load(g), b_comp(g)))
            for dco in range(16):
                tasks.append((c_load(dco), c_comp(dco)))
            for hf in range(2):
                for cb in range(4):
                    tasks.append((d_load(hf, cb), d_comp(hf, cb)))

        import os
        _mt = int(os.environ.get("KMAXTASK", "0"))
        if _mt:
            tasks = tasks[:_mt]
        wtasks = [i for i, t in enumerate(tasks) if t[0] is not None]
        slot_of = {ti: k % 2 for k, ti in enumerate(wtasks)}
        nxt = {wtasks[k]: wtasks[k + 1] for k in range(len(wtasks) - 1)}
        first = wtasks[0]
        tasks[first][0](WSL[slot_of[first]], WSd[slot_of[first]])
        for i, (ld, cp) in enumerate(tasks):
            if ld is not None:
                if i in nxt:
                    n = nxt[i]
                    tasks[n][0](WSL[slot_of[n]], WSd[slot_of[n]])
                try:
                    cp(WSL[slot_of[i]], WSd[slot_of[i]])
                except _Stop:
                    break
            else:
                cp(None, None)
        S.drain("sp")
        if os.environ.get("KDEBUG"):
            print({n: (len(e.ops), e.cnt) for n, e in S.eng.items()}, len(S.sems), flush=True)

        with nc.Block() as block:
            @block.tensor
            def _(e):
                for f in S.eng["pe"].ops:
                    f(e)

            @block.scalar
            def _(e):
                for f in S.eng["act"].ops:
                    f(e)

            @block.vector
            def _(e):
                for f in S.eng["dve"].ops:
                    f(e)

            @block.gpsimd
            def _(e):
                for f in S.eng["pool"].ops:
                    f(e)

            @block.sync
            def _(e):
                for f in S.eng["sp"].ops:
                    f(e)
    return nc


def host_inputs(x, c, w_ada, b_ada, norm_pre, norm_post, w_in, lower_bounds, hgrn_norm,
                pool_w, pool_scale, w_proj_a, w_proj_b, w_out):
    f = np.float32
    xs = np.ascontiguousarray(x, dtype=f).reshape(NCORE, T, D)
    fm = lambda v, n: np.ascontiguousarray(v.reshape(v.shape[0], n, 128).transpose(0, 2, 1), dtype=f)
    ident = np.eye(128, dtype=f)
    s_i = np.arange(128)[:, None]
    t_i = np.arange(128)[None, :]
    cmask = ((s_i // 32 == t_i // 32) & (t_i >= s_i)).astype(f)
    lb_fm = np.ascontiguousarray(np.asarray(lower_bounds, dtype=f).reshape(4, 8, 128).transpose(2, 1, 0)).reshape(128, 32)
    shared = {
        "w_ada": np.ascontiguousarray(w_ada, dtype=f), "w_in": np.ascontiguousarray(w_in, dtype=f),
        "w_pa": np.ascontiguousarray(w_proj_a, dtype=f), "w_pb": np.ascontiguousarray(w_proj_b, dtype=f),
        "w_out": np.ascontiguousarray(w_out, dtype=f), "pool_w": np.ascontiguousarray(pool_w, dtype=f),
        "bada_fm": fm(np.asarray(b_ada)[:, :4096], 32), "bada_gate": np.ascontiguousarray(np.asarray(b_ada)[:, 4096:], dtype=f),
        "npre_fm": fm(np.asarray(norm_pre), 16), "npost": np.ascontiguousarray(norm_post, dtype=f),
        "lb_fm": lb_fm, "hn_fm": fm(np.asarray(hgrn_norm), 8), "ps_fm": fm(np.asarray(pool_scale), 8),
        "ident": ident, "cmask": cmask,
    }
    maps = []
    for r in range(NCORE):
        b, k = r // 4, r % 4
        cvec = np.zeros((128, 96), f)
        for r4 in range(4):
            cvec[32 * r4:32 * r4 + 32, 80 + r4] = 1.0
        for j in range(NCORE):
            if j // 4 == b and j < r:
                cvec[:, j] = 1.0
            if j // 4 == b and j == r - 1:
                cvec[:, 8 + j] = 1.0
        for g, w in enumerate(WINS):
            for t in range(16):
                pos = k * T + t + 1
                cvec[:, 16 + g * 16 + t] = 1.0 / min(pos, w)
        m = dict(shared)
        m["x"] = xs[r]
        m["cfm"] = np.ascontiguousarray(np.asarray(c, dtype=f)[b].reshape(16, 128).T)
        m["cvec"] = cvec
        maps.append(m)
    return maps


_NC_CACHE = {}


def kernel(x, c, w_ada, b_ada, norm_pre, norm_post, w_in, lower_bounds, hgrn_norm,
           pool_w, pool_scale, w_proj_a, w_proj_b, w_out, _nl=4):
    maps = host_inputs(x, c, w_ada[:_nl], b_ada, norm_pre, norm_post, w_in[:_nl], lower_bounds, hgrn_norm,
                       pool_w[:_nl], pool_scale, w_proj_a[:_nl], w_proj_b[:_nl], w_out[:_nl])
    if _nl not in _NC_CACHE:
        _NC_CACHE[_nl] = build(_nl)
    nc = _NC_CACHE[_nl]
    res = run_bass_kernel_spmd(nc, maps, core_ids=list(range(NCORE)))
    o = np.stack([np.asarray(r["out"]) for r in res.results], axis=0)
    return o.reshape(2, 8192, D).astype(np.float32)
```

```python
import numpy as np
from contextlib import ExitStack
import concourse.bass as bass
import concourse.mybir as mybir
from concourse.bass_utils import run_bass_kernel_spmd

F32 = mybir.dt.float32
BF16 = mybir.dt.bfloat16
AF = mybir.ActivationFunctionType
ALU = mybir.AluOpType
AX = mybir.AxisListType

T = 2048
D = 2048
NCORE = 8
EPS = 1e-6
WINS = (2, 4, 8, 16)


class Dep:
    __slots__ = ("w", "r", "dsem", "dcnt")

    def __init__(self):
        self.w = None
        self.r = {}
        self.dsem = None
        self.dcnt = 0


class Eng:
    def __init__(self, name):
        self.name = name
        self.sem = None
        self.cnt = 0
        self.seen = {}
        self.ops = []


class Sched:
    def __init__(self, nc, stack):
        self.nc = nc
        self.stack = stack
        self.sems = []
        self.eng = {n: Eng(n) for n in ("pe", "act", "dve", "pool", "sp")}
        for n in ("pe", "act", "dve", "pool"):
            self.eng[n].sem = self.newsem("e_" + n)
        self.dma_deps = []
        self.named = {}

    def newsem(self, name):
        h = self.stack.enter_context(self.nc.semaphore(name))
        self.sems.append(h)
        return len(self.sems) - 1

    def dep(self, snapshot=False):
        d = Dep()
        if snapshot:
            for n in ("pe", "act", "dve", "pool"):
                e = self.eng[n]
                if e.cnt:
                    d.r[e.sem] = e.cnt
            for o in self.dma_deps:
                if o.dcnt:
                    d.r[o.dsem] = o.dcnt
        return d

    def _waits(self, e, reads, writes):
        need = {}

        def add(tok):
            if tok is not None and need.get(tok[0], 0) < tok[1]:
                need[tok[0]] = tok[1]

        for d in reads:
            add(d.w)
        for d in writes:
            add(d.w)
            for k, v in d.r.items():
                add((k, v))
        for si, v in need.items():
            if e.seen.get(si, 0) < v:
                e.seen[si] = v
                sem = self.sems[si]
                e.ops.append(lambda h, sem=sem, v=v: h.wait_ge(sem, v))

    def _mark(self, tok, reads, writes):
        for d in reads:
            if d.r.get(tok[0], 0) < tok[1]:
                d.r[tok[0]] = tok[1]
        for d in writes:
            d.w = tok
            d.r = {}

    def op(self, en, fn, reads=(), writes=()):
        e = self.eng[en]
        self._waits(e, reads, writes)
        e.cnt += 1
        tok = (e.sem, e.cnt)
        sem = self.sems[e.sem]
        e.ops.append(lambda h, fn=fn, sem=sem: fn(h).then_inc(sem, 1))
        self._mark(tok, reads, writes)

    def group(self, en, fns, reads=(), writes=()):
        e = self.eng[en]
        self._waits(e, reads, writes)
        for fn in fns[:-1]:
            e.ops.append(lambda h, fn=fn: fn(h))
        e.cnt += 1
        tok = (e.sem, e.cnt)
        sem = self.sems[e.sem]
        last = fns[-1]
        e.ops.append(lambda h, fn=last, sem=sem: fn(h).then_inc(sem, 1))
        self._mark(tok, reads, writes)

    def dma(self, en, fn, semdep, reads=(), writes=()):
        e = self.eng[en]
        self._waits(e, reads, writes)
        if semdep.dsem is None:
            semdep.dsem = self.newsem("d%d" % len(self.sems))
            self.dma_deps.append(semdep)
        semdep.dcnt += 16
        tok = (semdep.dsem, semdep.dcnt)
        sem = self.sems[semdep.dsem]
        e.ops.append(lambda h, fn=fn, sem=sem: fn(h).then_inc(sem, 16))
        self._mark(tok, reads, writes)

    def sd(self, name):
        if name not in self.named:
            self.named[name] = Dep()
        return self.named[name]

    def coll(self, en, fn, semdep, reads=(), writes=()):
        e = self.eng[en]
        self._waits(e, reads, writes)
        if semdep.dsem is None:
            semdep.dsem = self.newsem("c%d" % len(self.sems))
            self.dma_deps.append(semdep)
        semdep.dcnt += 1
        tok = (semdep.dsem, semdep.dcnt)
        sem = self.sems[semdep.dsem]
        e.ops.append(lambda h, fn=fn, sem=sem: fn(h).then_inc(sem))
        self._mark(tok, reads, writes)

    def drain(self, en):
        e = self.eng[en]
        for o in self.dma_deps:
            if o.dcnt and e.seen.get(o.dsem, 0) < o.dcnt:
                sem = self.sems[o.dsem]
                e.ops.append(lambda h, sem=sem, v=o.dcnt: h.wait_ge(sem, v))
        for n in ("pe", "act", "dve", "pool"):
            o = self.eng[n]
            if o.cnt:
                sem = self.sems[o.sem]
                e.ops.append(lambda h, sem=sem, v=o.cnt: h.wait_ge(sem, v))


class _Stop(Exception):
    pass


def chk(k):
    import os
    v = int(os.environ.get("KSTOP", "0"))
    if v and k >= v:
        raise _Stop()


def build(NL):
    nc = bass.Bass("TRN2", target_bir_lowering=False)
    dt_in = lambda name, shape: nc.dram_tensor(name, shape, F32, kind="ExternalInput").ap()
    x_in = dt_in("x", [T, D])
    cfm = dt_in("cfm", [128, 16])
    w_ada = dt_in("w_ada", [NL, D, 3 * D])
    w_in = dt_in("w_in", [NL, D, 10240])
    w_pa = dt_in("w_pa", [NL, 1024, D])
    w_pb = dt_in("w_pb", [NL, 1024, D])
    w_out = dt_in("w_out", [NL, D, D])
    pool_w = dt_in("pool_w", [NL, 4, 256, 256])
    bada_fm = dt_in("bada_fm", [4, 128, 32])
    bada_gate = dt_in("bada_gate", [4, D])
    npre_fm = dt_in("npre_fm", [4, 128, 16])
    npost = dt_in("npost", [4, D])
    lb_fm = dt_in("lb_fm", [128, 32])
    hn_fm = dt_in("hn_fm", [4, 128, 8])
    ps_fm = dt_in("ps_fm", [4, 128, 8])
    ident_in = dt_in("ident", [128, 128])
    cmask_in = dt_in("cmask", [128, 128])
    cvec_in = dt_in("cvec", [128, 96])
    out = nc.dram_tensor("out", [T, D], F32, kind="ExternalOutput").ap()
    mg_dram = nc.dram_tensor("mg_scr", [16, 128, T], BF16)
    agin = [[nc.dram_tensor("agin_%d_%d" % (l, h), [128, 130], F32) for h in range(9)] for l in range(NL)]
    agout = [[nc.dram_tensor("agout_%d_%d" % (l, h), [NCORE * 128, 130], F32) for h in range(9)] for l in range(NL)]

    stack = ExitStack()
    with stack:
        arena_t = stack.enter_context(nc.sbuf_tensor("arena", [128, 53000], F32))
        psum = [stack.enter_context(nc.psum_tensor("ps%d" % i, [128, 512], F32)) for i in range(7)]
        ps7 = stack.enter_context(nc.psum_tensor("ps7", [128, 1024], BF16))
        S = Sched(nc, stack)
        PSd = [S.dep() for _ in range(8)]
        PS = [p[:] for p in psum]
        PS7 = ps7[:]

        def f32v(off, n):
            return arena_t[:, off:off + n]

        def bf16v(off, n):
            return arena_t[:, off:off + n // 2].bitcast(BF16)

        O_HT, O_YA, O_YB, O_W0, O_W1, O_WK, O_MISC = 0, 16384, 24576, 32768, 36864, 40960, 47104
        hT = bf16v(O_HT, 16 * T).rearrange("p (a b) -> p a b", b=T)
        YA = bf16v(O_YA, 8 * T).rearrange("p (a b) -> p a b", b=T)
        YB = bf16v(O_YB, 8 * T).rearrange("p (a b) -> p a b", b=T)
        OUTH = f32v(O_YA, 8 * T).rearrange("p (a b) -> p a b", b=T)
        WSL = [bf16v(O_W0, 16 * 512).rearrange("p (a b) -> p a b", b=512),
               bf16v(O_W1, 16 * 512).rearrange("p (a b) -> p a b", b=512)]
        WSd = [S.dep(), S.dep()]

        mo = [O_MISC]

        def misc_f(n):
            o = mo[0]
            mo[0] += n
            assert mo[0] <= 53000
            return f32v(o, n)

        def misc_b(n):
            o = mo[0]
            mo[0] += n // 2
            assert mo[0] <= 53000
            return bf16v(o, n)

        IDENT = misc_b(128)
        ONESB = misc_b(128)
        CMASK = misc_f(128)
        ONESF = misc_f(512)
        CVEC = misc_f(96)
        CA = misc_f(16)
        CAb = misc_b(16)
        LBR = misc_f(32)
        LBS = misc_f(8)
        LBALL = misc_f(32).rearrange("p (h l) -> p h l", l=4)
        OML = misc_f(32).rearrange("p (h l) -> p h l", l=4)
        NOML = misc_f(32).rearrange("p (h l) -> p h l", l=4)
        C2 = misc_f(32).rearrange("p (h l) -> p h l", l=4)
        EPSC = misc_f(1)
        MODFM = misc_f(32)
        BADAF = misc_f(32)
        NPRE = misc_f(16)
        G1 = misc_f(16)
        HN = misc_f(8)
        PSC = misc_f(8)
        POOLW = misc_b(4 * 2 * 256).rearrange("p (g c d) -> p g c d", g=4, c=2)
        GP = misc_f(2048)
        CAREP = misc_b(16 * 128).rearrange("p (a b) -> p a b", b=128)
        cst = S.dep()
        lyr = S.dep()
        GPd = S.dep()
        modd = S.dep()
        POOLWd = S.dep()

        class Work:
            def __init__(self, regions):
                self.regions = [list(r) for r in regions]
                self.i = 0

            def _take(self, nwords):
                nwords = (nwords + 15) // 16 * 16
                while self.i < len(self.regions):
                    r = self.regions[self.i]
                    if r[0] + nwords <= r[1]:
                        o = r[0]
                        r[0] += nwords
                        return o
                    self.i += 1
                raise RuntimeError("work arena overflow")

            def f(self, n):
                return f32v(self._take(n), n)

            def b(self, n):
                return bf16v(self._take(n // 2), n)

        sp, pool = "sp", "pool"
        act = lambda out_, in_, func, reads, writes, **kw: S.op(
            "act", lambda e: e.activation(out=out_, in_=in_, func=func, **kw), reads, writes)
        V = lambda fn, reads, writes: S.op("dve", fn, reads, writes)

        def mm(o, l, r, st, sp_):
            return lambda e: e.matmul(o, l, r, start=st, stop=sp_)

        S.dma(pool, lambda e: e.dma_start(out=IDENT, in_=ident_in), cst, writes=[cst])
        S.dma(sp, lambda e: e.dma_start(out=CMASK, in_=cmask_in), cst, writes=[cst])
        S.dma(sp, lambda e: e.dma_start(out=CVEC, in_=cvec_in), cst, writes=[cst])
        S.dma(sp, lambda e: e.dma_start(out=CA, in_=cfm), cst, writes=[cst])
        S.dma(sp, lambda e: e.dma_start(out=LBR, in_=lb_fm), cst, writes=[cst])
        c2 = S.dep()
        V(lambda e: e.memset(ONESB, 1.0), [], [c2])
        V(lambda e: e.memset(ONESF, 1.0), [], [c2])
        V(lambda e: e.memset(EPSC, EPS), [], [c2])
        LBR3 = LBR.rearrange("p (h l) -> p h l", l=4)
        act(LBR, LBR, AF.Exp, [cst], [c2])
        V(lambda e: e.reduce_sum(out=LBS, in_=LBR3, axis=AX.X), [c2], [c2])
        V(lambda e: e.reciprocal(out=LBS, in_=LBS), [c2], [c2])
        V(lambda e: e.tensor_tensor(out=LBR3, in0=LBR3, in1=LBS.unsqueeze(2).to_broadcast([128, 8, 4]), op=ALU.mult), [c2], [c2])
        V(lambda e: e.memset(LBALL[:, :, 0:1], 0.0), [], [c2])
        V(lambda e: e.tensor_copy(out=LBALL[:, :, 1:2], in_=LBR3[:, :, 1:2]), [c2], [c2])
        V(lambda e: e.tensor_tensor(out=LBALL[:, :, 2:3], in0=LBALL[:, :, 1:2], in1=LBR3[:, :, 2:3], op=ALU.add), [c2], [c2])
        V(lambda e: e.tensor_tensor(out=LBALL[:, :, 3:4], in0=LBALL[:, :, 2:3], in1=LBR3[:, :, 3:4], op=ALU.add), [c2], [c2])
        V(lambda e: e.tensor_scalar(out=OML, in0=LBALL, scalar1=-1.0, scalar2=1.0, op0=ALU.mult, op1=ALU.add), [c2], [c2])
        V(lambda e: e.tensor_scalar(out=NOML, in0=LBALL, scalar1=1.0, scalar2=-1.0, op0=ALU.mult, op1=ALU.add), [c2], [c2])
        V(lambda e: e.tensor_scalar(out=C2, in0=LBALL, scalar1=-1.0, scalar2=1e-30, op0=ALU.mult, op1=ALU.add), [c2], [c2])
        act(CA, CA, AF.Silu, [cst], [c2])
        V(lambda e: e.tensor_copy(out=CAb, in_=CA), [c2], [c2])
        V(lambda e: e.tensor_copy(out=CAREP, in_=CAb.unsqueeze(2).to_broadcast([128, 16, 128])), [c2], [c2])
        CST = [cst, c2]

        hTd = [S.dep() for _ in range(4)]
        YAd = [[S.dep() for _ in range(4)] for _ in range(8)]
        YBd = [[S.dep() for _ in range(4)] for _ in range(8)]
        outd = [S.dep() for _ in range(16)]
        mgd = [S.dep() for _ in range(16)]

        tasks = []

        def wrows(ap2d):
            return ap2d.rearrange("(a p) c -> p a c", p=128)

        for l in range(NL):
            x_src = x_in if l == 0 else out

            def m_load(jb, l=l):
                def f(W, Wd):
                    S.dma(pool, lambda e: e.dma_start(out=W, in_=wrows(w_ada[l, :, jb * 512:(jb + 1) * 512])), Wd, writes=[Wd])
                return f

            def m_comp(jb, l=l):
                def f(W, Wd):
                    if jb == 0:
                        S.dma(sp, lambda e: e.dma_start(out=BADAF, in_=bada_fm[l]), lyr, writes=[lyr, modd])
                        S.dma(sp, lambda e: e.dma_start(out=NPRE, in_=npre_fm[l]), lyr, writes=[lyr])
                        S.dma(sp, lambda e: e.dma_start(out=HN, in_=hn_fm[l]), lyr, writes=[lyr])
                        S.dma(sp, lambda e: e.dma_start(out=PSC, in_=ps_fm[l]), lyr, writes=[lyr])
                        S.dma(pool, lambda e: e.dma_start(out=POOLW, in_=pool_w[l].rearrange("g (c p) d -> p g c d", p=128)), POOLWd, writes=[POOLWd])
                    if jb < 8:
                        fns = []
                        for cc in range(4):
                            col = jb * 4 + cc
                            for dc in range(16):
                                fns.append(mm(PS[0][:, col:col + 1], W[:, dc, cc * 128:(cc + 1) * 128], CAb[:, dc:dc + 1], dc == 0, dc == 15))
                        S.group("pe", fns, [Wd] + CST, [PSd[0]])
                        if jb == 7:
                            V(lambda e: e.tensor_tensor(out=MODFM, in0=PS[0][:, 0:32], in1=BADAF, op=ALU.add), [PSd[0], lyr], [modd])
                            V(lambda e: e.scalar_tensor_tensor(out=G1, in0=MODFM[:, 16:32], scalar=1.0, in1=NPRE, op0=ALU.add, op1=ALU.mult), [modd, lyr], [modd])
                    else:
                        wk = Work([(O_WK, O_MISC)])
                        BR = wk.f(512)
                        NR = wk.f(512)
                        brd = S.dep(snapshot=True)
                        q = jb - 8
                        S.dma(sp, lambda e: e.dma_start(out=BR, in_=bada_gate[l:l + 1, q * 512:(q + 1) * 512].to_broadcast([128, 512])), S.sd("br"), writes=[brd])
                        S.dma(sp, lambda e: e.dma_start(out=NR, in_=npost[l:l + 1, q * 512:(q + 1) * 512].to_broadcast([128, 512])), S.sd("br"), writes=[brd])
                        fns = [mm(PS[1], CAREP[:, dc, :], W[:, dc, :], dc == 0, dc == 15) for dc in range(16)]
                        S.group("pe", fns, [Wd] + CST, [PSd[1]])
                        gsl = GP[:, q * 512:(q + 1) * 512]
                        V(lambda e: e.tensor_tensor(out=BR, in0=PS[1], in1=BR, op=ALU.add), [PSd[1], brd], [brd])
                        V(lambda e: e.tensor_tensor(out=gsl, in0=BR, in1=NR, op=ALU.mult), [brd], [GPd])
                return f

            def p0_comp(l=l, x_src=x_src):
                def f(W, Wd):
                    wk = Work([(O_WK, O_MISC), (O_YB, O_W0)])
                    XT = [wk.f(2048), wk.f(2048)]
                    XN = wk.b(2048)
                    SS = wk.f(16)
                    JK = wk.b(2048)
                    XTd = [S.dep(snapshot=True), S.dep(snapshot=True)]
                    XNd = S.dep(snapshot=True)
                    ssd = S.dep(snapshot=True)
                    for tt in range(16):
                        xt, xd = XT[tt % 2], XTd[tt % 2]
                        tg = tt // 4
                        S.dma(sp, lambda e, xt=xt, tt=tt: e.dma_start(out=xt, in_=x_src[tt * 128:(tt + 1) * 128, :]), S.sd("xt%d" % (tt % 2)),
                              reads=[outd[tt]], writes=[xd])
                        sc = SS[:, tt:tt + 1]
                        V(lambda e, sc=sc: e.memset(sc, 0.0), [], [ssd])
                        act(JK, xt, AF.Square, [xd], [ssd], accum_out=sc)
                        act(sc, sc, AF.Sqrt, [ssd], [ssd], scale=1.0 / D, bias=EPSC)
                        V(lambda e, sc=sc: e.reciprocal(out=sc, in_=sc), [ssd], [ssd])
                        act(XN, xt, AF.Identity, [xd, ssd], [XNd], scale=sc)
                        for half in range(2):
                            fns = [lambda e, j=j, half=half: e.transpose(PS7[:, j * 128:(j + 1) * 128], XN[:, (half * 8 + j) * 128:(half * 8 + j + 1) * 128], IDENT) for j in range(8)]
                            S.group("pe", fns, [XNd] + CST, [PSd[7]])
                            for j in range(8):
                                dc = half * 8 + j
                                act(hT[:, dc, tt * 128:(tt + 1) * 128], PS7[:, j * 128:(j + 1) * 128], AF.Identity,
                                    [PSd[7], modd], [hTd[tg]], scale=G1[:, dc:dc + 1], bias=MODFM[:, dc:dc + 1])
                return f

            def a_load(h, l=l):
                def f(W, Wd):
                    for i, base in enumerate((0, 1024, 2048, 3072)):
                        c0 = base + h * 128
                        S.dma(pool, lambda e, i=i, c0=c0: e.dma_start(out=W[:, :, i * 128:(i + 1) * 128], in_=wrows(w_in[l, :, c0:c0 + 128])), Wd, writes=[Wd])
                return f

            astate = {}

            def a_comp(h, l=l, astate=astate):
                def f(W, Wd):
                    nd = lambda: S.dep(snapshot=True)
                    st = astate
                    if h == 0:
                        wk = Work([(O_YB, O_W0), (O_WK, O_MISC)])
                        st.clear()
                        st["OL"] = wk.f(2048); st["OLd"] = [nd() for _ in range(4)]
                        st["QH"] = [wk.b(2048), wk.b(2048)]; st["QHd"] = [[nd() for _ in range(4)] for _ in range(2)]
                        st["SZ"] = wk.b(2048); st["SZd"] = [nd() for _ in range(4)]
                        st["F1"] = wk.f(512); st["F1d"] = nd()
                        st["F2"] = wk.f(512); st["F2d"] = nd()
                        for nm in ("SQ", "F3", "F4", "F5", "F6"):
                            st[nm] = wk.b(512); st[nm + "d"] = nd()
                        for nm in ("QT", "KT", "VT"):
                            st[nm] = [wk.b(512), wk.b(512)]; st[nm + "d"] = [nd(), nd()]
                        st["BM"] = [wk.f(32), wk.f(32)]; st["BMd"] = [nd(), nd()]
                        st["EE"] = [wk.f(32), wk.f(32)]; st["EEd"] = [nd(), nd()]
                        st["KTTf"] = [wk.b(512) for _ in range(4)]; st["KTTd"] = nd()
                        st["SC"] = wk.b(512).rearrange("p (a b) -> p a b", b=128); st["SCd"] = nd()
                        st["ST"] = wk.f(128); st["STd"] = [nd(), nd()]
                        st["ST2"] = wk.f(128)
                        st["KVS"] = wk.f(512); st["KVSd"] = nd()
                        st["SB"] = wk.b(16 * 128).rearrange("p (a b) -> p a b", b=128); st["SBd"] = nd()
                        st["AGS"] = wk.f(130); st["AGSd"] = nd()
                        st["AGR"] = wk.f(8 * 130).rearrange("p (a b) -> p a b", b=130); st["AGRd"] = nd()
                        st["ACC"] = wk.f(128); st["T1"] = wk.f(128); st["DP"] = wk.f(8); st["ACCd"] = nd()
                        st["SINB"] = wk.b(128); st["SINd"] = nd()
                        st["SQO"] = st["F3"]; st["SQOd"] = st["F3d"]
                        st["RS"] = st["F2"]; st["RSd"] = st["F2d"]
                    OL, OLd, SZ, SZd = st["OL"], st["OLd"], st["SZ"], st["SZd"]
                    F1, F1d, F2, F2d = st["F1"], st["F1d"], st["F2"], st["F2d"]
                    SQ, SQd, F3, F3d, F4, F4d, F5, F5d, F6, F6d = (st[k] for k in ("SQ", "SQd", "F3", "F3d", "F4", "F4d", "F5", "F5d", "F6", "F6d"))
                    KTTf, KTTd, SC, SCd, ST, STd, SB, SBd = (st[k] for k in ("KTTf", "KTTd", "SC", "SCd", "ST", "STd", "SB", "SBd"))
                    KTT = [k_.rearrange("p (a b) -> p a b", b=128) for k_ in KTTf]
                    AGS, AGSd, AGR, AGRd, ACC, T1, DP, ACCd = (st[k] for k in ("AGS", "AGSd", "AGR", "AGRd", "ACC", "T1", "DP", "ACCd"))
                    SINB, SINd, SQO, SQOd, RS, RSd = (st[k] for k in ("SINB", "SINd", "SQO", "SQOd", "RS", "RSd"))
                    hp = h % 2
                    lbs = LBALL[:, h, l:l + 1]; oml = OML[:, h, l:l + 1]; noml = NOML[:, h, l:l + 1]; c2s = C2[:, h, l:l + 1]
                    PL = lambda fn, reads, writes: S.op("dve", fn, reads, writes)

                    ST2, KVS, KVSd = st["ST2"], st["KVS"], st["KVSd"]

                    def G(tg):
                        tok = slice(tg * 512, (tg + 1) * 512)
                        steps = []
                        for (bank, c0) in ((0, 0), (1, 128)):
                            steps.append(lambda bank=bank, c0=c0: S.group(
                                "pe", [mm(PS[bank], W[:, dc, c0:c0 + 128], hT[:, dc, tok], dc == 0, dc == 15) for dc in range(16)],
                                [Wd, hTd[tg]], [PSd[bank]]))

                        def gv():
                            fns = []
                            for ti in range(4):
                                t0 = tg * 512 + ti * 128
                                for dc in range(16):
                                    fns.append(mm(PS[3][:, ti * 128:(ti + 1) * 128], hT[:, dc, t0:t0 + 128], W[:, dc, 256:384], dc == 0, dc == 15))
                            S.group("pe", fns, [Wd, hTd[tg]], [PSd[3]])
                        steps.append(gv)
                        return steps

                    def Erest(tg):
                        par = tg % 2
                        tok = slice(tg * 512, (tg + 1) * 512)
                        QT, QTd, KT, KTd = st["QT"][par], st["QTd"][par], st["KT"][par], st["KTd"][par]
                        VTf, VTd = st["VT"][par], st["VTd"][par]
                        BM, BMd, EE, EEd = st["BM"][par], st["BMd"][par], st["EE"][par], st["EEd"][par]
                        QH, QHd = st["QH"][hp], st["QHd"][hp]
                        steps = []
                        ap = steps.append
                        ap(lambda: act(F1, PS[1], AF.Sigmoid, [PSd[1]], [F1d]))
                        ap(lambda: act(SQ, PS[0], AF.Silu, [PSd[0]], [SQd]))
                        ap(lambda: act(VTf, PS[3], AF.Copy, [PSd[3]], [VTd]))
                        ap(lambda: PL(lambda e: e.tensor_scalar(out=F5, in0=F1, scalar1=noml, scalar2=oml, op0=ALU.mult, op1=ALU.add), [F1d] + CST, [F5d]))
                        ap(lambda: V(lambda e: e.tensor_scalar(out=F1, in0=F1, scalar1=oml, scalar2=c2s, op0=ALU.mult, op1=ALU.max), CST, [F1d]))
                        ap(lambda: act(F1, F1, AF.Ln, CST, [F1d], bias=lbs, scale=1.0))
                        if tg > 0:
                            BMp, BMpd = st["BM"][1 - par], st["BMd"][1 - par]
                            ap(lambda: V(lambda e: e.tensor_copy(out=BM[:, 0:1], in_=BMp[:, 17:18]), [BMpd], [BMd]))
                        else:
                            ap(lambda: V(lambda e: e.memset(BM[:, 0:1], 0.0), [], [BMd]))
                        ap(lambda: V(lambda e: e.tensor_tensor_scan(out=F2, data0=ONESF, data1=F1, initial=BM[:, 0:1], op0=ALU.mult, op1=ALU.add),
                                     [F1d, BMd] + CST, [F2d]))
                        ap(lambda: V(lambda e: e.tensor_copy(out=BM[:, 1:17], in_=F2[:, 15:512:32]), [F2d], [BMd]))
                        ap(lambda: V(lambda e: e.tensor_copy(out=BM[:, 17:18], in_=F2[:, 511:512]), [F2d], [BMd]))
                        ap(lambda: V(lambda e: e.tensor_tensor(out=EE[:, 0:17], in0=BM[:, 1:18], in1=BM[:, 0:17], op=ALU.subtract), [BMd], [EEd]))
                        ap(lambda: act(EE[:, 0:17], EE[:, 0:17], AF.Exp, [], [EEd]))
                        ap(lambda: V(lambda e: e.tensor_tensor(out=F1.rearrange("p (a b) -> p a b", b=32), in0=F2.rearrange("p (a b) -> p a b", b=32),
                                                               in1=BM[:, 1:17].unsqueeze(2).to_broadcast([128, 16, 32]), op=ALU.subtract), [F2d, BMd], [F1d]))
                        ap(lambda: V(lambda e: e.tensor_scalar(out=F1, in0=F1, scalar1=40.0, scalar2=-40.0, op0=ALU.min, op1=ALU.max), [], [F1d]))
                        ap(lambda: act(F3, F1, AF.Exp, [F1d], [F3d]))
                        ap(lambda: act(F4, F1, AF.Exp, [F1d], [F4d], scale=-1.0))
                        ap(lambda: act(F6, F2, AF.Exp, [F2d], [F6d]))
                        ap(lambda: PL(lambda e: e.tensor_tensor(out=KT, in0=F5, in1=F4, op=ALU.mult), [F5d, F4d], [KTd]))
                        ap(lambda: PL(lambda e: e.tensor_tensor(out=QT, in0=SQ, in1=F3, op=ALU.mult), [SQd, F3d], [QTd]))
                        ap(lambda: PL(lambda e: e.tensor_tensor(out=QH[:, tok], in0=SQ, in1=F6, op=ALU.mult), [SQd, F6d], [QHd[tg]]))
                        return steps

                    def L(tg):
                        par = tg % 2
                        tok = slice(tg * 512, (tg + 1) * 512)
                        QT, QTd, KT, KTd = st["QT"][par], st["QTd"][par], st["KT"][par], st["KTd"][par]
                        VTf, VTd = st["VT"][par], st["VTd"][par]
                        VT = VTf.rearrange("p (a b) -> p a b", b=128)
                        BM, BMd, EE, EEd = st["BM"][par], st["BMd"][par], st["EE"][par], st["EEd"][par]
                        steps = []
                        ap = steps.append
                        STS = [ST, ST2]
                        if tg == 0:
                            ap(lambda: V(lambda e: e.memset(ST, 0.0), [], [STd[0]]))
                        ap(lambda: S.group("pe", [lambda e, ti=ti: e.transpose(PS7[:, ti * 128:(ti + 1) * 128], KT[:, ti * 128:(ti + 1) * 128], IDENT) for ti in range(4)],
                                           [KTd] + CST, [PSd[7]]))
                        for r4 in range(4):
                            ap(lambda r4=r4: act(KTTf[r4], PS7[:, 0:512], AF.Identity, [PSd[7]] + CST, [KTTd], scale=CVEC[:, 80 + r4:81 + r4]))
                        ap(lambda: S.group("pe", [mm(PS[4][:, ti * 128:(ti + 1) * 128], KT[:, ti * 128:(ti + 1) * 128], QT[:, ti * 128:(ti + 1) * 128], True, True) for ti in range(4)],
                                           [KTd, QTd], [PSd[4]]))
                        ap(lambda: V(lambda e: e.tensor_tensor(out=SC, in0=PS[4].rearrange("p (a b) -> p a b", b=128),
                                                               in1=CMASK.unsqueeze(1).to_broadcast([128, 4, 128]), op=ALU.mult), [PSd[4]] + CST, [SCd]))
                        ap(lambda: V(lambda e: e.tensor_scalar(out=ST, in0=ST, scalar1=EE[:, 0:1], scalar2=None, op0=ALU.mult), [EEd], [STd[0]]))
                        ap(lambda: act(SB[:, 0, :], ST, AF.Copy, [STd[0]], [SBd]))
                        for rnd in range(4):
                            bank = 5 + rnd % 2

                            def kvr(rnd=rnd, bank=bank):
                                fns = []
                                for cc in range(4):
                                    c = rnd * 4 + cc
                                    fns.append(mm(PS[bank][:, cc * 128:(cc + 1) * 128], KTT[c % 4][:, c // 4, :], VT[:, c // 4, :], True, True))
                                S.group("pe", fns, [KTTd, VTd], [PSd[bank]])
                            ap(kvr)
                            for cc in range(4):
                                c = rnd * 4 + cc
                                ap(lambda bank=bank, cc=cc, c=c: act(KVS[:, cc * 128:(cc + 1) * 128], PS[bank][:, cc * 128:(cc + 1) * 128], AF.Identity,
                                                                     [PSd[bank], EEd], [KVSd], scale=EE[:, c + 1:c + 2]))
                            for cc in range(4):
                                c = rnd * 4 + cc
                                so, sn = STS[c % 2], STS[(c + 1) % 2]
                                ap(lambda c=c, cc=cc, so=so, sn=sn: V(lambda e: e.scalar_tensor_tensor(out=sn, in0=so, scalar=EE[:, c + 1:c + 2], in1=KVS[:, cc * 128:(cc + 1) * 128],
                                                                                                    op0=ALU.mult, op1=ALU.add), [KVSd, EEd, STd[c % 2]], [STd[(c + 1) % 2]]))
                                if c < 15:
                                    ap(lambda c=c, sn=sn: act(SB[:, c + 1, :], sn, AF.Copy, [STd[(c + 1) % 2]], [SBd]))

                        def omm():
                            fns = []
                            for c in range(16):
                                ti, r0 = c // 4, (c % 4) * 32
                                cols = slice(c * 32, (c + 1) * 32)
                                fns.append(mm(PS[2][:, cols], VT[:, ti, :], SC[:, ti, r0:r0 + 32], True, False))
                                fns.append(mm(PS[2][:, cols], SB[:, c, :], QT[:, cols], False, True))
                            S.group("pe", fns, [VTd, SCd, SBd, QTd], [PSd[2]])
                        ap(omm)
                        ap(lambda: act(OL[:, tok], PS[2], AF.Copy, [PSd[2]], [OLd[tg]]))
                        if tg == 3:
                            def agx():
                                V(lambda e: e.tensor_copy(out=AGS[:, 0:128], in_=ST), [STd[0]], [AGSd])
                                act(AGS[:, 128:130], BM[:, 17:18].to_broadcast([128, 2]), AF.Exp, [BMd], [AGSd])
                                gi, go = agin[l][h], agout[l][h]
                                gid, god = S.dep(), S.dep()
                                S.dma(sp, lambda e: e.dma_start(out=gi[:, :], in_=AGS), S.sd("agi"), reads=[AGSd], writes=[gid])
                                S.coll(pool, lambda e: e.collective_compute("AllGather", ALU.bypass, replica_groups=[list(range(NCORE))],
                                                                            ins=[gi.ap().opt()], outs=[go.ap().opt()]), S.sd("ago"), reads=[gid], writes=[god])
                                S.dma(sp, lambda e: e.dma_start(out=AGR, in_=go.ap().rearrange("(r p) f -> p r f", p=128)), S.sd("agr"), reads=[god], writes=[AGRd])
                            ap(agx)
                            for tz in range(4):
                                tkz = slice(tz * 512, (tz + 1) * 512)
                                ap(lambda tz=tz, tkz=tkz: S.group("pe", [mm(PS[3], W[:, dc, 384:512], hT[:, dc, tkz], dc == 0, dc == 15) for dc in range(16)],
                                                                  [Wd, hTd[tz]], [PSd[3]]))
                                ap(lambda tz=tz, tkz=tkz: act(SZ[:, tkz], PS[3], AF.Silu, [PSd[3]], [SZd[tz]]))
                        return steps

                    def merge(Ls, Bs=(), Cs=(), cpos=(14, 28, 42)):
                        bi = ci = 0
                        n = len(Ls)
                        for i, a_ in enumerate(Ls):
                            a_()
                            while bi < len(Bs) and (bi + 1) * n <= (i + 1) * len(Bs):
                                Bs[bi]()
                                bi += 1
                            while ci < len(Cs) and i >= cpos[min(ci, len(cpos) - 1)]:
                                Cs[ci]()
                                ci += 1
                        for k in range(bi, len(Bs)):
                            Bs[k]()
                        for k in range(ci, len(Cs)):
                            Cs[k]()

                    def P(hh):
                        QH, QHd = st["QH"][hh % 2], st["QHd"][hh % 2]
                        V(lambda e: e.memset(ACC, 0.0), [], [ACCd])
                        for j in range(8):
                            mj = CVEC[:, j:j + 1]
                            V(lambda e, j=j, mj=mj: e.tensor_scalar(out=DP[:, j:j + 1], in0=AGR[:, j, 128:129], scalar1=-1.0, scalar2=mj, op0=ALU.add, op1=ALU.mult),
                              [AGRd] + CST, [ACCd])
                            V(lambda e, j=j: e.tensor_scalar(out=T1, in0=ACC, scalar1=DP[:, j:j + 1], scalar2=None, op0=ALU.mult), [], [ACCd])
                            V(lambda e: e.tensor_tensor(out=ACC, in0=ACC, in1=T1, op=ALU.add), [], [ACCd])
                            V(lambda e, j=j, mj=mj: e.scalar_tensor_tensor(out=ACC, in0=AGR[:, j, 0:128], scalar=mj, in1=ACC, op0=ALU.mult, op1=ALU.add), [AGRd], [ACCd])
                        V(lambda e: e.tensor_copy(out=SINB, in_=ACC), [ACCd], [SINd])
                        for tg in range(4):
                            tok = slice(tg * 512, (tg + 1) * 512)
                            S.group("pe", [mm(PS[4], SINB, QH[:, tok], True, True)], [SINd, QHd[tg]], [PSd[4]])
                            V(lambda e, tok=tok: e.tensor_tensor(out=OL[:, tok], in0=PS[4], in1=OL[:, tok], op=ALU.add), [PSd[4]], [OLd[tg]])
                            act(SQO, OL[:, tok], AF.Square, [OLd[tg]], [SQOd])
                            S.group("pe", [mm(PS[5], ONESB, SQO, True, True)], [SQOd] + CST, [PSd[5]])
                            act(RS, PS[5], AF.Sqrt, [PSd[5]] + CST, [RSd], scale=1.0 / 128, bias=EPSC)
                            V(lambda e: e.reciprocal(out=RS, in_=RS), [], [RSd])
                            V(lambda e, tok=tok: e.tensor_tensor(out=OL[:, tok], in0=OL[:, tok], in1=RS, op=ALU.mult), [RSd], [OLd[tg]])
                            V(lambda e, tok=tok, hh=hh: e.scalar_tensor_tensor(out=YA[:, hh, tok], in0=OL[:, tok], scalar=HN[:, hh:hh + 1], in1=SZ[:, tok], op0=ALU.mult, op1=ALU.mult),
                              [OLd[tg], SZd[tg], lyr], [YAd[hh][tg]])

                    merge(G(0))
                    merge(Erest(0), Cs=G(1), cpos=(2, 8, 14))
                    if h > 0:
                        P(h - 1)
                    merge(L(0), Erest(1), G(2))
                    merge(L(1), Erest(2), G(3))
                    merge(L(2), Erest(3))
                    merge(L(3))
                    if h == 7:
                        P(7)
                return f

            bstate = {}

            def b_load(g, l=l):
                def f(W, Wd):
                    S.dma(pool, lambda e: e.dma_start(out=W[:, :, 0:256], in_=wrows(w_in[l, :, 4096 + g * 256:4096 + (g + 1) * 256])), Wd, writes=[Wd])
                    S.dma(pool, lambda e: e.dma_start(out=W[:, :, 256:512], in_=wrows(w_in[l, :, 5120 + g * 256:5120 + (g + 1) * 256])), Wd, writes=[Wd])
                return f

            def b_comp(g, l=l, bstate=bstate):
                def f(W, Wd):
                    nd = lambda: S.dep(snapshot=True)
                    if g == 0:
                        wk = Work([(O_WK, O_MISC)])
                        bstate["wk"] = wk
                        bstate["FIRST"] = wk.f(8 * 32).rearrange("p (a b) -> p a b", b=32)
                        bstate["HAL"] = wk.f(130)
                        bstate["SZF"] = wk.f(8 * 16).rearrange("p (a b) -> p a b", b=16)
                        bstate["VBE"] = [wk.f(528), wk.f(528)]
                        bstate["TA"] = wk.f(528)
                        bstate["TB"] = wk.f(528)
                        bstate["PBT"] = [wk.b(512), wk.b(512)]
                        for k in ("FIRSTd", "HALd", "SZFd", "TAd", "PBTd0", "PBTd1", "VBEd0", "VBEd1"):
                            bstate[k] = nd()
                    wk = bstate["wk"]
                    FIRST, HAL, SZF, VBE, TA, TB, PBT = (bstate[k] for k in ("FIRST", "HAL", "SZF", "VBE", "TA", "TB", "PBT"))
                    FIRSTd, HALd, SZFd, TAd = (bstate[k] for k in ("FIRSTd", "HALd", "SZFd", "TAd"))
                    PBTd = [bstate["PBTd0"], bstate["PBTd1"]]
                    VBEd = [bstate["VBEd0"], bstate["VBEd1"]]
                    w = WINS[g]
                    for q in range(2):
                        V(lambda e, q=q: e.memset(VBE[q][:, 0:16], 0.0), [], [VBEd[q]])
                    for tg in range(4):
                        tok = slice(tg * 512, (tg + 1) * 512)
                        for q in range(4):
                            S.group("pe", [mm(PS[q], W[:, dc, q * 128:(q + 1) * 128], hT[:, dc, tok], dc == 0, dc == 15) for dc in range(16)],
                                    [Wd, hTd[tg]], [PSd[q]])
                        for q in range(2):
                            j = g * 2 + q
                            act(VBE[q][:, 16:528], PS[q], AF.Copy, [PSd[q]], [VBEd[q]])
                            act(YB[:, j, tok], PS[2 + q], AF.Silu, [PSd[2 + q]], [YBd[j][tg]])
                            if tg == 0:
                                V(lambda e, q=q, j=j: e.tensor_copy(out=FIRST[:, j, 16:32], in_=VBE[q][:, 16:32]), [VBEd[q]], [FIRSTd])
                                V(lambda e, j=j: e.tensor_copy(out=SZF[:, j, :], in_=YB[:, j, 0:16]), [YBd[j][0]], [SZFd])
                            src = VBE[q]
                            sh = 1
                            lo = 0
                            bufs = [TA, TB]
                            bi = 0
                            while sh < w:
                                dst = bufs[bi]
                                lo += sh
                                V(lambda e, src=src, dst=dst, sh=sh, lo=lo: e.tensor_tensor(out=dst[:, lo:528], in0=src[:, lo:528], in1=src[:, lo - sh:528 - sh], op=ALU.add),
                                  [VBEd[q]], [TAd])
                                src = dst
                                bi ^= 1
                                sh *= 2
                            V(lambda e, src=src, q=q: e.scalar_tensor_tensor(out=PBT[q], in0=src[:, 16:528], scalar=1.0 / w, in1=VBE[q][:, 16:528], op0=ALU.mult, op1=ALU.subtract),
                              [TAd, VBEd[q]], [PBTd[q]])
                            if tg == 3:
                                V(lambda e, q=q, j=j: e.tensor_copy(out=HAL[:, j * 16:(j + 1) * 16], in_=VBE[q][:, 512:528]), [VBEd[q]], [HALd])
                            else:
                                V(lambda e, q=q: e.tensor_copy(out=VBE[q][:, 0:16], in_=VBE[q][:, 512:528]), [TAd, PBTd[q]], [VBEd[q]])
                        for q in range(2):
                            j = g * 2 + q
                            S.group("pe", [mm(PS[4 + q], POOLW[:, g, cq, q * 128:(q + 1) * 128], PBT[cq], cq == 0, cq == 1) for cq in range(2)],
                                    [PBTd[0], PBTd[1], POOLWd], [PSd[4 + q]])
                            V(lambda e, q=q, j=j, tok=tok: e.scalar_tensor_tensor(out=YB[:, j, tok], in0=PS[4 + q], scalar=PSC[:, j:j + 1], in1=YB[:, j, tok], op0=ALU.mult, op1=ALU.mult),
                              [PSd[4 + q], lyr], [YBd[j][tg]])
                    if g == 3:
                        AGR = wk.f(8 * 130).rearrange("p (a b) -> p a b", b=130)
                        AGRd = nd()
                        PF = wk.b(8 * 16).rearrange("p (a b) -> p a b", b=16)
                        PFd = nd()
                        FA = wk.f(8 * 32).rearrange("p (a b) -> p a b", b=32)
                        FB = wk.f(8 * 32).rearrange("p (a b) -> p a b", b=32)
                        FAd = nd()
                        gi, go = agin[l][8], agout[l][8]
                        gid, god = S.dep(), S.dep()
                        V(lambda e: e.memset(HAL[:, 128:130], 0.0), [], [HALd])
                        S.dma(sp, lambda e: e.dma_start(out=gi[:, :], in_=HAL), S.sd("agi"), reads=[HALd], writes=[gid])
                        S.coll(pool, lambda e: e.collective_compute("AllGather", ALU.bypass, replica_groups=[list(range(NCORE))],
                                                                    ins=[gi.ap().opt()], outs=[go.ap().opt()]), S.sd("ago"), reads=[gid], writes=[god])
                        S.dma(sp, lambda e: e.dma_start(out=AGR, in_=go.ap().rearrange("(r p) f -> p r f", p=128)), S.sd("agr"), reads=[god], writes=[AGRd])
                        V(lambda e: e.memset(FIRST[:, :, 0:16], 0.0), [], [FIRSTd])
                        for r in range(8):
                            V(lambda e, r=r: e.scalar_tensor_tensor(out=FIRST[:, :, 0:16], in0=AGR[:, r, 0:128].rearrange("p (a b) -> p a b", b=16),
                                                                    scalar=CVEC[:, 8 + r:9 + r], in1=FIRST[:, :, 0:16], op0=ALU.mult, op1=ALU.add),
                              [AGRd] + CST, [FIRSTd])
                        for gg in range(4):
                            ww = WINS[gg]
                            js = slice(gg * 2, gg * 2 + 2)
                            src = FIRST
                            sh = 1
                            lo = 0
                            bufs = [FA, FB]
                            bi = 0
                            while sh < ww:
                                dst = bufs[bi]
                                lo += sh
                                V(lambda e, src=src, dst=dst, sh=sh, js=js, lo=lo: e.tensor_tensor(out=dst[:, js, lo:32], in0=src[:, js, lo:32], in1=src[:, js, lo - sh:32 - sh], op=ALU.add),
                                  [FIRSTd], [FAd])
                                src = dst
                                bi ^= 1
                                sh *= 2
                            rd = CVEC[:, 16 + gg * 16:32 + gg * 16]
                            V(lambda e, src=src, js=js, rd=rd: e.tensor_tensor(out=src[:, js, 16:32], in0=src[:, js, 16:32], in1=rd.unsqueeze(1).to_broadcast([128, 2, 16]), op=ALU.mult),
                              CST, [FAd])
                            V(lambda e, src=src, js=js: e.tensor_tensor(out=PF[:, js, :], in0=src[:, js, 16:32], in1=FIRST[:, js, 16:32], op=ALU.subtract), [FAd, FIRSTd], [PFd])
                        for gg in range(4):
                            for q in range(2):
                                j = gg * 2 + q
                                S.group("pe", [mm(PS[4 + q][:, 0:16], POOLW[:, gg, cq, q * 128:(q + 1) * 128], PF[:, gg * 2 + cq, :], cq == 0, cq == 1) for cq in range(2)],
                                        [PFd, POOLWd], [PSd[4 + q]])
                                V(lambda e, q=q, j=j: e.scalar_tensor_tensor(out=YB[:, j, 0:16], in0=PS[4 + q][:, 0:16], scalar=PSC[:, j:j + 1], in1=SZF[:, j, :], op0=ALU.mult, op1=ALU.mult),
                                  [PSd[4 + q], SZFd, lyr], [YBd[j][0]])
                return f

            cstate = {}

            def c_load(dco, l=l):
                def f(W, Wd):
                    c0 = dco * 128
                    S.dma(pool, lambda e: e.dma_start(out=W[:, :, 0:128], in_=wrows(w_in[l, :, 6144 + c0:6144 + c0 + 128])), Wd, writes=[Wd])
                    S.dma(pool, lambda e: e.dma_start(out=W[:, :, 128:256], in_=wrows(w_in[l, :, 8192 + c0:8192 + c0 + 128])), Wd, writes=[Wd])
                    S.dma(pool, lambda e: e.dma_start(out=W[:, 0:8, 256:384], in_=wrows(w_pa[l, :, c0:c0 + 128])), Wd, writes=[Wd])
                    S.dma(pool, lambda e: e.dma_start(out=W[:, 0:8, 384:512], in_=wrows(w_pb[l, :, c0:c0 + 128])), Wd, writes=[Wd])
                return f

            def c_comp(dco, l=l, cstate=cstate):
                def f(W, Wd):
                    nd = lambda: S.dep(snapshot=True)
                    if dco == 0:
                        wk = Work([(O_WK, O_MISC)])
                        cstate["MG"] = [wk.b(2048), wk.b(2048)]
                        cstate["MGd"] = [nd(), nd()]
                        cstate["SA"] = wk.f(512); cstate["SB_"] = wk.f(512)
                        cstate["SAd"] = nd(); cstate["SBd"] = nd()
                    MG, MGd = cstate["MG"][dco % 2], cstate["MGd"][dco % 2]
                    SA, SBb, SAd, SBd2 = cstate["SA"], cstate["SB_"], cstate["SAd"], cstate["SBd"]
                    for tg in range(4):
                        tok = slice(tg * 512, (tg + 1) * 512)
                        S.group("pe", [mm(PS[0], W[:, dc, 0:128], hT[:, dc, tok], dc == 0, dc == 15) for dc in range(16)], [Wd, hTd[tg]], [PSd[0]])
                        S.group("pe", [mm(PS[1], W[:, dc, 128:256], hT[:, dc, tok], dc == 0, dc == 15) for dc in range(16)], [Wd, hTd[tg]], [PSd[1]])
                        S.group("pe", [mm(PS[2], W[:, kc, 256:384], YA[:, kc, tok], kc == 0, kc == 7) for kc in range(8)],
                                [Wd] + [YAd[kc][tg] for kc in range(8)], [PSd[2]])
                        S.group("pe", [mm(PS[3], W[:, kc, 384:512], YB[:, kc, tok], kc == 0, kc == 7) for kc in range(8)],
                                [Wd] + [YBd[kc][tg] for kc in range(8)], [PSd[3]])
                        act(SA, PS[0], AF.Sigmoid, [PSd[0]], [SAd])
                        act(SBb, PS[1], AF.Sigmoid, [PSd[1]], [SBd2])
                        V(lambda e: e.tensor_tensor(out=SA, in0=PS[2], in1=SA, op=ALU.mult), [PSd[2]], [SAd])
                        V(lambda e: e.tensor_tensor(out=SBb, in0=PS[3], in1=SBb, op=ALU.mult), [PSd[3]], [SBd2])
                        V(lambda e, tok=tok, MG=MG: e.tensor_tensor(out=MG[:, tok], in0=SA, in1=SBb, op=ALU.add), [SAd, SBd2], [MGd])
                    S.dma(sp, lambda e, MG=MG: e.dma_start(out=mg_dram[dco], in_=MG), S.sd("mg%d" % (dco % 2)), reads=[MGd], writes=[mgd[dco]])
                return f

            dstate = {}

            def d_load(hf, cb, l=l):
                def f(W, Wd):
                    S.dma(pool, lambda e: e.dma_start(out=W, in_=wrows(w_out[l, :, cb * 512:(cb + 1) * 512])), Wd, writes=[Wd])
                return f

            def d_comp(hf, cb, l=l, dstate=dstate, x_src=x_src):
                def f(W, Wd):
                    nd = lambda: S.dep(snapshot=True)
                    if hf == 0 and cb == 0:
                        for dc in range(16):
                            S.dma(sp, lambda e, dc=dc: e.dma_start(out=hT[:, dc, :], in_=mg_dram[dc]), S.sd("mgld"),
                                  reads=[mgd[dc]], writes=hTd)
                        wk = Work([(O_WK, O_MISC)])
                        dstate["XT"] = [wk.f(2048), wk.f(2048)]
                        dstate["XTd"] = [nd(), nd()]
                        dstate["JK"] = wk.b(512); dstate["JKd"] = nd()
                        dstate["SS"] = wk.f(64).rearrange("p (a b) -> p a b", b=4); dstate["SSd"] = nd()
                        dstate["RS"] = wk.f(16)
                        dstate["OUTd"] = [nd() for _ in range(8)]
                    XT, XTd, JK, JKd, SS, SSd, RSs, OUTd = (dstate[k] for k in ("XT", "XTd", "JK", "JKd", "SS", "SSd", "RS", "OUTd"))
                    for t8 in range(8):
                        tt = hf * 8 + t8
                        S.group("pe", [mm(PS[t8 % 4], hT[:, dc, tt * 128:(tt + 1) * 128], W[:, dc, :], dc == 0, dc == 15) for dc in range(16)],
                                [Wd] + hTd, [PSd[t8 % 4]])
                        act(OUTH[:, t8, cb * 512:(cb + 1) * 512], PS[t8 % 4], AF.Copy, [PSd[t8 % 4]], [OUTd[t8]])
                        V(lambda e, tt=tt: e.memset(SS[:, tt, cb:cb + 1], 0.0), [], [SSd])
                        act(JK, PS[t8 % 4], AF.Square, [PSd[t8 % 4]], [JKd, SSd], accum_out=SS[:, tt, cb:cb + 1])
                        if cb == 3:
                            xt, xd = XT[tt % 2], XTd[tt % 2]
                            S.dma(sp, lambda e, xt=xt, tt=tt: e.dma_start(out=xt, in_=x_src[tt * 128:(tt + 1) * 128, :]), S.sd("xt%d" % (tt % 2)), reads=[outd[tt]], writes=[xd])
                            rs = RSs[:, tt:tt + 1]
                            V(lambda e, tt=tt, rs=rs: e.reduce_sum(out=rs, in_=SS[:, tt, :], axis=AX.X), [SSd], [SSd])
                            act(rs, rs, AF.Sqrt, CST, [SSd], scale=1.0 / D, bias=EPSC)
                            V(lambda e, rs=rs: e.reciprocal(out=rs, in_=rs), [], [SSd])
                            V(lambda e, t8=t8, rs=rs: e.scalar_tensor_tensor(out=OUTH[:, t8, :], in0=OUTH[:, t8, :], scalar=rs, in1=GP, op0=ALU.mult, op1=ALU.mult),
                              [SSd, GPd], [OUTd[t8]])
                            V(lambda e, t8=t8, xt=xt: e.tensor_tensor(out=xt, in0=OUTH[:, t8, :], in1=xt, op=ALU.add), [OUTd[t8]], [xd])
                            S.dma(sp, lambda e, xt=xt, tt=tt: e.dma_start(out=out[tt * 128:(tt + 1) * 128, :], in_=xt), S.sd("xt%d" % (tt % 2)), reads=[xd], writes=[outd[tt]])
                return f

            for jb in range(8):
                tasks.append((m_load(jb), m_comp(jb)))
            tasks.append((None, p0_comp()))
            for jb in range(8, 12):
                tasks.append((m_load(jb), m_comp(jb)))
            for h in range(8):
                tasks.append((a_load(h), a_comp(h)))
            for g in range(4):
                tasks.append((b_load(g), b_comp(g)))
            for dco in range(16):
                tasks.append((c_load(dco), c_comp(dco)))
            for hf in range(2):
                for cb in range(4):
                    tasks.append((d_load(hf, cb), d_comp(hf, cb)))

        import os
        _mt = int(os.environ.get("KMAXTASK", "0"))
        if _mt:
            tasks = tasks[:_mt]
        wtasks = [i for i, t in enumerate(tasks) if t[0] is not None]
        slot_of = {ti: k % 2 for k, ti in enumerate(wtasks)}
        nxt = {wtasks[k]: wtasks[k + 1] for k in range(len(wtasks) - 1)}
        first = wtasks[0]
        tasks[first][0](WSL[slot_of[first]], WSd[slot_of[first]])
        for i, (ld, cp) in enumerate(tasks):
            if ld is not None:
                if i in nxt:
                    n = nxt[i]
                    tasks[n][0](WSL[slot_of[n]], WSd[slot_of[n]])
                try:
                    cp(WSL[slot_of[i]], WSd[slot_of[i]])
                except _Stop:
                    break
            else:
                cp(None, None)
        S.drain("sp")
        if os.environ.get("KDEBUG"):
            print({n: (len(e.ops), e.cnt) for n, e in S.eng.items()}, len(S.sems), flush=True)

        with nc.Block() as block:
            @block.tensor
            def _(e):
                for f in S.eng["pe"].ops:
                    f(e)

            @block.scalar
            def _(e):
                for f in S.eng["act"].ops:
                    f(e)

            @block.vector
            def _(e):
                for f in S.eng["dve"].ops:
                    f(e)

            @block.gpsimd
            def _(e):
                for f in S.eng["pool"].ops:
                    f(e)

            @block.sync
            def _(e):
                for f in S.eng["sp"].ops:
                    f(e)
    return nc


def host_inputs(x, c, w_ada, b_ada, norm_pre, norm_post, w_in, lower_bounds, hgrn_norm,
                pool_w, pool_scale, w_proj_a, w_proj_b, w_out):
    f = np.float32
    xs = np.ascontiguousarray(x, dtype=f).reshape(NCORE, T, D)
    fm = lambda v, n: np.ascontiguousarray(v.reshape(v.shape[0], n, 128).transpose(0, 2, 1), dtype=f)
    ident = np.eye(128, dtype=f)
    s_i = np.arange(128)[:, None]
    t_i = np.arange(128)[None, :]
    cmask = ((s_i // 32 == t_i // 32) & (t_i >= s_i)).astype(f)
    lb_fm = np.ascontiguousarray(np.asarray(lower_bounds, dtype=f).reshape(4, 8, 128).transpose(2, 1, 0)).reshape(128, 32)
    shared = {
        "w_ada": np.ascontiguousarray(w_ada, dtype=f), "w_in": np.ascontiguousarray(w_in, dtype=f),
        "w_pa": np.ascontiguousarray(w_proj_a, dtype=f), "w_pb": np.ascontiguousarray(w_proj_b, dtype=f),
        "w_out": np.ascontiguousarray(w_out, dtype=f), "pool_w": np.ascontiguousarray(pool_w, dtype=f),
        "bada_fm": fm(np.asarray(b_ada)[:, :4096], 32), "bada_gate": np.ascontiguousarray(np.asarray(b_ada)[:, 4096:], dtype=f),
        "npre_fm": fm(np.asarray(norm_pre), 16), "npost": np.ascontiguousarray(norm_post, dtype=f),
        "lb_fm": lb_fm, "hn_fm": fm(np.asarray(hgrn_norm), 8), "ps_fm": fm(np.asarray(pool_scale), 8),
        "ident": ident, "cmask": cmask,
    }
    maps = []
    for r in range(NCORE):
        b, k = r // 4, r % 4
        cvec = np.zeros((128, 96), f)
        for r4 in range(4):
            cvec[32 * r4:32 * r4 + 32, 80 + r4] = 1.0
        for j in range(NCORE):
            if j // 4 == b and j < r:
                cvec[:, j] = 1.0
            if j // 4 == b and j == r - 1:
                cvec[:, 8 + j] = 1.0
        for g, w in enumerate(WINS):
            for t in range(16):
                pos = k * T + t + 1
                cvec[:, 16 + g * 16 + t] = 1.0 / min(pos, w)
        m = dict(shared)
        m["x"] = xs[r]
        m["cfm"] = np.ascontiguousarray(np.asarray(c, dtype=f)[b].reshape(16, 128).T)
        m["cvec"] = cvec
        maps.append(m)
    return maps


_NC_CACHE = {}


def kernel(x, c, w_ada, b_ada, norm_pre, norm_post, w_in, lower_bounds, hgrn_norm,
           pool_w, pool_scale, w_proj_a, w_proj_b, w_out, _nl=4):
    maps = host_inputs(x, c, w_ada[:_nl], b_ada, norm_pre, norm_post, w_in[:_nl], lower_bounds, hgrn_norm,
                       pool_w[:_nl], pool_scale, w_proj_a[:_nl], w_proj_b[:_nl], w_out[:_nl])
    if _nl not in _NC_CACHE:
        _NC_CACHE[_nl] = build(_nl)
    nc = _NC_CACHE[_nl]
    res = run_bass_kernel_spmd(nc, maps, core_ids=list(range(NCORE)))
    o = np.stack([np.asarray(r["out"]) for r in res.results], axis=0)
    return o.reshape(2, 8192, D).astype(np.float32)
```

```python
import numpy as np
from contextlib import ExitStack
import concourse.bass as bass
import concourse.mybir as mybir
from concourse.bass_utils import run_bass_kernel_spmd

F32 = mybir.dt.float32
BF16 = mybir.dt.bfloat16
AF = mybir.ActivationFunctionType
ALU = mybir.AluOpType
AX = mybir.AxisListType

T = 2048
D = 2048
NCORE = 8
EPS = 1e-6
WINS = (2, 4, 8, 16)


class Dep:
    __slots__ = ("w", "r", "dsem", "dcnt")

    def __init__(self):
        self.w = None
        self.r = {}
        self.dsem = None
        self.dcnt = 0


class Eng:
    def __init__(self, name):
        self.name = name
        self.sem = None
        self.cnt = 0
        self.seen = {}
        self.ops = []


class Sched:
    def __init__(self, nc, stack):
        self.nc = nc
        self.stack = stack
        self.sems = []
        self.eng = {n: Eng(n) for n in ("pe", "act", "dve", "pool", "sp")}
        for n in ("pe", "act", "dve", "pool"):
            self.eng[n].sem = self.newsem("e_" + n)
        self.dma_deps = []
        self.named = {}

    def newsem(self, name):
        h = self.stack.enter_context(self.nc.semaphore(name))
        self.sems.append(h)
        return len(self.sems) - 1

    def dep(self, snapshot=False):
        d = Dep()
        if snapshot:
            for n in ("pe", "act", "dve", "pool"):
                e = self.eng[n]
                if e.cnt:
                    d.r[e.sem] = e.cnt
            for o in self.dma_deps:
                if o.dcnt:
                    d.r[o.dsem] = o.dcnt
        return d

    def _waits(self, e, reads, writes):
        need = {}

        def add(tok):
            if tok is not None and need.get(tok[0], 0) < tok[1]:
                need[tok[0]] = tok[1]

        for d in reads:
            add(d.w)
        for d in writes:
            add(d.w)
            for k, v in d.r.items():
                add((k, v))
        for si, v in need.items():
            if e.seen.get(si, 0) < v:
                e.seen[si] = v
                sem = self.sems[si]
                e.ops.append(lambda h, sem=sem, v=v: h.wait_ge(sem, v))

    def _mark(self, tok, reads, writes):
        for d in reads:
            if d.r.get(tok[0], 0) < tok[1]:
                d.r[tok[0]] = tok[1]
        for d in writes:
            d.w = tok
            d.r = {}

    def op(self, en, fn, reads=(), writes=()):
        e = self.eng[en]
        self._waits(e, reads, writes)
        e.cnt += 1
        tok = (e.sem, e.cnt)
        sem = self.sems[e.sem]
        e.ops.append(lambda h, fn=fn, sem=sem: fn(h).then_inc(sem, 1))
        self._mark(tok, reads, writes)

    def group(self, en, fns, reads=(), writes=()):
        e = self.eng[en]
        self._waits(e, reads, writes)
        for fn in fns[:-1]:
            e.ops.append(lambda h, fn=fn: fn(h))
        e.cnt += 1
        tok = (e.sem, e.cnt)
        sem = self.sems[e.sem]
        last = fns[-1]
        e.ops.append(lambda h, fn=last, sem=sem: fn(h).then_inc(sem, 1))
        self._mark(tok, reads, writes)

    def dma(self, en, fn, semdep, reads=(), writes=()):
        e = self.eng[en]
        self._waits(e, reads, writes)
        if semdep.dsem is None:
            semdep.dsem = self.newsem("d%d" % len(self.sems))
            self.dma_deps.append(semdep)
        semdep.dcnt += 16
        tok = (semdep.dsem, semdep.dcnt)
        sem = self.sems[semdep.dsem]
        e.ops.append(lambda h, fn=fn, sem=sem: fn(h).then_inc(sem, 16))
        self._mark(tok, reads, writes)

    def sd(self, name):
        if name not in self.named:
            self.named[name] = Dep()
        return self.named[name]

    def coll(self, en, fn, semdep, reads=(), writes=()):
        e = self.eng[en]
        self._waits(e, reads, writes)
        if semdep.dsem is None:
            semdep.dsem = self.newsem("c%d" % len(self.sems))
            self.dma_deps.append(semdep)
        semdep.dcnt += 1
        tok = (semdep.dsem, semdep.dcnt)
        sem = self.sems[semdep.dsem]
        e.ops.append(lambda h, fn=fn, sem=sem: fn(h).then_inc(sem))
        self._mark(tok, reads, writes)

    def drain(self, en):
        e = self.eng[en]
        for o in self.dma_deps:
            if o.dcnt and e.seen.get(o.dsem, 0) < o.dcnt:
                sem = self.sems[o.dsem]
                e.ops.append(lambda h, sem=sem, v=o.dcnt: h.wait_ge(sem, v))
        for n in ("pe", "act", "dve", "pool"):
            o = self.eng[n]
            if o.cnt:
                sem = self.sems[o.sem]
                e.ops.append(lambda h, sem=sem, v=o.cnt: h.wait_ge(sem, v))


class _Stop(Exception):
    pass


def chk(k):
    import os
    v = int(os.environ.get("KSTOP", "0"))
    if v and k >= v:
        raise _Stop()


def build(NL):
    nc = bass.Bass("TRN2", target_bir_lowering=False)
    dt_in = lambda name, shape: nc.dram_tensor(name, shape, F32, kind="ExternalInput").ap()
    x_in = dt_in("x", [T, D])
    cfm = dt_in("cfm", [128, 16])
    w_ada = dt_in("w_ada", [NL, D, 3 * D])
    w_in = dt_in("w_in", [NL, D, 10240])
    w_pa = dt_in("w_pa", [NL, 1024, D])
    w_pb = dt_in("w_pb", [NL, 1024, D])
    w_out = dt_in("w_out", [NL, D, D])
    pool_w = dt_in("pool_w", [NL, 4, 256, 256])
    bada_fm = dt_in("bada_fm", [4, 128, 32])
    bada_gate = dt_in("bada_gate", [4, D])
    npre_fm = dt_in("npre_fm", [4, 128, 16])
    npost = dt_in("npost", [4, D])
    lb_fm = dt_in("lb_fm", [128, 32])
    hn_fm = dt_in("hn_fm", [4, 128, 8])
    ps_fm = dt_in("ps_fm", [4, 128, 8])
    ident_in = dt_in("ident", [128, 128])
    cmask_in = dt_in("cmask", [128, 128])
    cvec_in = dt_in("cvec", [128, 96])
    out = nc.dram_tensor("out", [T, D], F32, kind="ExternalOutput").ap()
    mg_dram = nc.dram_tensor("mg_scr", [16, 128, T], BF16)
    agin = [[nc.dram_tensor("agin_%d_%d" % (l, h), [128, 130], F32) for h in range(9)] for l in range(NL)]
    agout = [[nc.dram_tensor("agout_%d_%d" % (l, h), [NCORE * 128, 130], F32) for h in range(9)] for l in range(NL)]

    stack = ExitStack()
    with stack:
        arena_t = stack.enter_context(nc.sbuf_tensor("arena", [128, 53000], F32))
        psum = [stack.enter_context(nc.psum_tensor("ps%d" % i, [128, 512], F32)) for i in range(7)]
        ps7 = stack.enter_context(nc.psum_tensor("ps7", [128, 1024], BF16))
        S = Sched(nc, stack)
        PSd = [S.dep() for _ in range(8)]
        PS = [p[:] for p in psum]
        PS7 = ps7[:]

        def f32v(off, n):
            return arena_t[:, off:off + n]

        def bf16v(off, n):
            return arena_t[:, off:off + n // 2].bitcast(BF16)

        O_HT, O_YA, O_YB, O_W0, O_W1, O_WK, O_MISC = 0, 16384, 24576, 32768, 36864, 40960, 47104
        hT = bf16v(O_HT, 16 * T).rearrange("p (a b) -> p a b", b=T)
        YA = bf16v(O_YA, 8 * T).rearrange("p (a b) -> p a b", b=T)
        YB = bf16v(O_YB, 8 * T).rearrange("p (a b) -> p a b", b=T)
        OUTH = f32v(O_YA, 8 * T).rearrange("p (a b) -> p a b", b=T)
        WSL = [bf16v(O_W0, 16 * 512).rearrange("p (a b) -> p a b", b=512),
               bf16v(O_W1, 16 * 512).rearrange("p (a b) -> p a b", b=512)]
        WSd = [S.dep(), S.dep()]

        mo = [O_MISC]

        def misc_f(n):
            o = mo[0]
            mo[0] += n
            assert mo[0] <= 53000
            return f32v(o, n)

        def misc_b(n):
            o = mo[0]
            mo[0] += n // 2
            assert mo[0] <= 53000
            return bf16v(o, n)

        IDENT = misc_b(128)
        ONESB = misc_b(128)
        CMASK = misc_f(128)
        ONESF = misc_f(512)
        CVEC = misc_f(96)
        CA = misc_f(16)
        CAb = misc_b(16)
        LBR = misc_f(32)
        LBS = misc_f(8)
        LBALL = misc_f(32).rearrange("p (h l) -> p h l", l=4)
        OML = misc_f(32).rearrange("p (h l) -> p h l", l=4)
        NOML = misc_f(32).rearrange("p (h l) -> p h l", l=4)
        C2 = misc_f(32).rearrange("p (h l) -> p h l", l=4)
        EPSC = misc_f(1)
        MODFM = misc_f(32)
        BADAF = misc_f(32)
        NPRE = misc_f(16)
        G1 = misc_f(16)
        HN = misc_f(8)
        PSC = misc_f(8)
        POOLW = misc_b(4 * 2 * 256).rearrange("p (g c d) -> p g c d", g=4, c=2)
        GP = misc_f(2048)
        CAREP = misc_b(16 * 128).rearrange("p (a b) -> p a b", b=128)
        cst = S.dep()
        lyr = S.dep()
        GPd = S.dep()
        modd = S.dep()
        POOLWd = S.dep()

        class Work:
            def __init__(self, regions):
                self.regions = [list(r) for r in regions]
                self.i = 0

            def _take(self, nwords):
                nwords = (nwords + 15) // 16 * 16
                while self.i < len(self.regions):
                    r = self.regions[self.i]
                    if r[0] + nwords <= r[1]:
                        o = r[0]
                        r[0] += nwords
                        return o
                    self.i += 1
                raise RuntimeError("work arena overflow")

            def f(self, n):
                return f32v(self._take(n), n)

            def b(self, n):
                return bf16v(self._take(n // 2), n)

        sp, pool = "sp", "pool"
        act = lambda out_, in_, func, reads, writes, **kw: S.op(
            "act", lambda e: e.activation(out=out_, in_=in_, func=func, **kw), reads, writes)
        V = lambda fn, reads, writes: S.op("dve", fn, reads, writes)

        def mm(o, l, r, st, sp_):
            return lambda e: e.matmul(o, l, r, start=st, stop=sp_)

        S.dma(pool, lambda e: e.dma_start(out=IDENT, in_=ident_in), cst, writes=[cst])
        S.dma(sp, lambda e: e.dma_start(out=CMASK, in_=cmask_in), cst, writes=[cst])
        S.dma(sp, lambda e: e.dma_start(out=CVEC, in_=cvec_in), cst, writes=[cst])
        S.dma(sp, lambda e: e.dma_start(out=CA, in_=cfm), cst, writes=[cst])
        S.dma(sp, lambda e: e.dma_start(out=LBR, in_=lb_fm), cst, writes=[cst])
        c2 = S.dep()
        V(lambda e: e.memset(ONESB, 1.0), [], [c2])
        V(lambda e: e.memset(ONESF, 1.0), [], [c2])
        V(lambda e: e.memset(EPSC, EPS), [], [c2])
        LBR3 = LBR.rearrange("p (h l) -> p h l", l=4)
        act(LBR, LBR, AF.Exp, [cst], [c2])
        V(lambda e: e.reduce_sum(out=LBS, in_=LBR3, axis=AX.X), [c2], [c2])
        V(lambda e: e.reciprocal(out=LBS, in_=LBS), [c2], [c2])
        V(lambda e: e.tensor_tensor(out=LBR3, in0=LBR3, in1=LBS.unsqueeze(2).to_broadcast([128, 8, 4]), op=ALU.mult), [c2], [c2])
        V(lambda e: e.memset(LBALL[:, :, 0:1], 0.0), [], [c2])
        V(lambda e: e.tensor_copy(out=LBALL[:, :, 1:2], in_=LBR3[:, :, 1:2]), [c2], [c2])
        V(lambda e: e.tensor_tensor(out=LBALL[:, :, 2:3], in0=LBALL[:, :, 1:2], in1=LBR3[:, :, 2:3], op=ALU.add), [c2], [c2])
        V(lambda e: e.tensor_tensor(out=LBALL[:, :, 3:4], in0=LBALL[:, :, 2:3], in1=LBR3[:, :, 3:4], op=ALU.add), [c2], [c2])
        V(lambda e: e.tensor_scalar(out=OML, in0=LBALL, scalar1=-1.0, scalar2=1.0, op0=ALU.mult, op1=ALU.add), [c2], [c2])
        V(lambda e: e.tensor_scalar(out=NOML, in0=LBALL, scalar1=1.0, scalar2=-1.0, op0=ALU.mult, op1=ALU.add), [c2], [c2])
        V(lambda e: e.tensor_scalar(out=C2, in0=LBALL, scalar1=-1.0, scalar2=1e-30, op0=ALU.mult, op1=ALU.add), [c2], [c2])
        act(CA, CA, AF.Silu, [cst], [c2])
        V(lambda e: e.tensor_copy(out=CAb, in_=CA), [c2], [c2])
        V(lambda e: e.tensor_copy(out=CAREP, in_=CAb.unsqueeze(2).to_broadcast([128, 16, 128])), [c2], [c2])
        CST = [cst, c2]

        hTd = [S.dep() for _ in range(4)]
        YAd = [[S.dep() for _ in range(4)] for _ in range(8)]
        YBd = [[S.dep() for _ in range(4)] for _ in range(8)]
        outd = [S.dep() for _ in range(16)]
        mgd = [S.dep() for _ in range(16)]

        tasks = []

        def wrows(ap2d):
            return ap2d.rearrange("(a p) c -> p a c", p=128)

        for l in range(NL):
            x_src = x_in if l == 0 else out

            def m_load(jb, l=l):
                def f(W, Wd):
                    S.dma(pool, lambda e: e.dma_start(out=W, in_=wrows(w_ada[l, :, jb * 512:(jb + 1) * 512])), Wd, writes=[Wd])
                return f

            def m_comp(jb, l=l):
                def f(W, Wd):
                    if jb == 0:
                        S.dma(sp, lambda e: e.dma_start(out=BADAF, in_=bada_fm[l]), lyr, writes=[lyr, modd])
                        S.dma(sp, lambda e: e.dma_start(out=NPRE, in_=npre_fm[l]), lyr, writes=[lyr])
                        S.dma(sp, lambda e: e.dma_start(out=HN, in_=hn_fm[l]), lyr, writes=[lyr])
                        S.dma(sp, lambda e: e.dma_start(out=PSC, in_=ps_fm[l]), lyr, writes=[lyr])
                        S.dma(pool, lambda e: e.dma_start(out=POOLW, in_=pool_w[l].rearrange("g (c p) d -> p g c d", p=128)), POOLWd, writes=[POOLWd])
                    if jb < 8:
                        fns = []
                        for cc in range(4):
                            col = jb * 4 + cc
                            for dc in range(16):
                                fns.append(mm(PS[0][:, col:col + 1], W[:, dc, cc * 128:(cc + 1) * 128], CAb[:, dc:dc + 1], dc == 0, dc == 15))
                        S.group("pe", fns, [Wd] + CST, [PSd[0]])
                        if jb == 7:
                            V(lambda e: e.tensor_tensor(out=MODFM, in0=PS[0][:, 0:32], in1=BADAF, op=ALU.add), [PSd[0], lyr], [modd])
                            V(lambda e: e.scalar_tensor_tensor(out=G1, in0=MODFM[:, 16:32], scalar=1.0, in1=NPRE, op0=ALU.add, op1=ALU.mult), [modd, lyr], [modd])
                    else:
                        wk = Work([(O_WK, O_MISC)])
                        BR = wk.f(512)
                        NR = wk.f(512)
                        brd = S.dep(snapshot=True)
                        q = jb - 8
                        S.dma(sp, lambda e: e.dma_start(out=BR, in_=bada_gate[l:l + 1, q * 512:(q + 1) * 512].to_broadcast([128, 512])), S.sd("br"), writes=[brd])
                        S.dma(sp, lambda e: e.dma_start(out=NR, in_=npost[l:l + 1, q * 512:(q + 1) * 512].to_broadcast([128, 512])), S.sd("br"), writes=[brd])
                        fns = [mm(PS[1], CAREP[:, dc, :], W[:, dc, :], dc == 0, dc == 15) for dc in range(16)]
                        S.group("pe", fns, [Wd] + CST, [PSd[1]])
                        gsl = GP[:, q * 512:(q + 1) * 512]
                        V(lambda e: e.tensor_tensor(out=BR, in0=PS[1], in1=BR, op=ALU.add), [PSd[1], brd], [brd])
                        V(lambda e: e.tensor_tensor(out=gsl, in0=BR, in1=NR, op=ALU.mult), [brd], [GPd])
                return f

            def p0_comp(l=l, x_src=x_src):
                def f(W, Wd):
                    wk = Work([(O_WK, O_MISC), (O_YB, O_W0)])
                    XT = [wk.f(2048), wk.f(2048)]
                    XN = wk.b(2048)
                    SS = wk.f(16)
                    JK = wk.b(2048)
                    HTMP = wk.f(1024).rearrange("p (a b) -> p a b", b=128)
                    htd = S.dep(snapshot=True)
                    XTd = [S.dep(snapshot=True), S.dep(snapshot=True)]
                    XNd = S.dep(snapshot=True)
                    ssd = S.dep(snapshot=True)
                    for tt in range(16):
                        xt, xd = XT[tt % 2], XTd[tt % 2]
                        tg = tt // 4
                        S.dma(sp, lambda e, xt=xt, tt=tt: e.dma_start(out=xt, in_=x_src[tt * 128:(tt + 1) * 128, :]), S.sd("xt%d" % (tt % 2)),
                              reads=[outd[tt]], writes=[xd])
                        sc = SS[:, tt:tt + 1]
                        V(lambda e, sc=sc: e.memset(sc, 0.0), [], [ssd])
                        act(JK, xt, AF.Square, [xd], [ssd], accum_out=sc)
                        act(sc, sc, AF.Sqrt, [ssd], [ssd], scale=1.0 / D, bias=EPSC)
                        V(lambda e, sc=sc: e.reciprocal(out=sc, in_=sc), [ssd], [ssd])
                        act(XN, xt, AF.Identity, [xd, ssd], [XNd], scale=sc)
                        for half in range(2):
                            fns = [lambda e, j=j, half=half: e.transpose(PS7[:, j * 128:(j + 1) * 128], XN[:, (half * 8 + j) * 128:(half * 8 + j + 1) * 128], IDENT) for j in range(8)]
                            S.group("pe", fns, [XNd] + CST, [PSd[7]])
                            hsl = hT[:, half * 8:half * 8 + 8, tt * 128:(tt + 1) * 128]
                            V(lambda e, half=half: e.tensor_tensor(out=HTMP, in0=PS7.rearrange("p (a b) -> p a b", b=128),
                                                                   in1=G1[:, half * 8:half * 8 + 8].unsqueeze(2).to_broadcast([128, 8, 128]), op=ALU.mult),
                              [PSd[7], modd], [htd])
                            V(lambda e, half=half, hsl=hsl: e.tensor_tensor(out=hsl, in0=HTMP,
                                                                            in1=MODFM[:, half * 8:half * 8 + 8].unsqueeze(2).to_broadcast([128, 8, 128]), op=ALU.add),
                              [htd, modd], [hTd[tg]])
                return f

            def a_load(h, l=l):
                def f(W, Wd):
                    for i, base in enumerate((0, 1024, 2048, 3072)):
                        c0 = base + h * 128
                        S.dma(pool, lambda e, i=i, c0=c0: e.dma_start(out=W[:, :, i * 128:(i + 1) * 128], in_=wrows(w_in[l, :, c0:c0 + 128])), Wd, writes=[Wd])
                return f

            astate = {}

            def a_comp(h, l=l, astate=astate):
                def f(W, Wd):
                    nd = lambda: S.dep(snapshot=True)
                    st = astate
                    if h == 0:
                        wk = Work([(O_YB, O_W0), (O_WK, O_MISC)])
                        st.clear()
                        st["OL"] = wk.f(2048); st["OLd"] = [nd() for _ in range(4)]
                        st["QH"] = [wk.b(2048), wk.b(2048)]; st["QHd"] = [[nd() for _ in range(4)] for _ in range(2)]
                        st["SZ"] = wk.b(2048); st["SZd"] = [nd() for _ in range(4)]
                        st["F1"] = wk.f(512); st["F1d"] = nd()
                        st["F2"] = wk.f(512); st["F2d"] = nd()
                        for nm in ("SQ", "F3", "F4", "F5", "F6"):
                            st[nm] = wk.b(512); st[nm + "d"] = nd()
                        for nm in ("QT", "KT", "VT"):
                            st[nm] = [wk.b(512), wk.b(512)]; st[nm + "d"] = [nd(), nd()]
                        st["BM"] = [wk.f(32), wk.f(32)]; st["BMd"] = [nd(), nd()]
                        st["EE"] = [wk.f(32), wk.f(32)]; st["EEd"] = [nd(), nd()]
                        st["KTTf"] = [wk.b(512) for _ in range(4)]; st["KTTd"] = nd()
                        st["SC"] = wk.b(512).rearrange("p (a b) -> p a b", b=128); st["SCd"] = nd()
                        st["ST"] = wk.f(128); st["STd"] = [nd(), nd()]
                        st["ST2"] = wk.f(128)
                        st["KVS"] = wk.f(512); st["KVSd"] = nd()
                        st["SB"] = wk.b(16 * 128).rearrange("p (a b) -> p a b", b=128); st["SBd"] = nd()
                        st["AGS"] = wk.f(130); st["AGSd"] = nd()
                        st["AGR"] = wk.f(8 * 130).rearrange("p (a b) -> p a b", b=130); st["AGRd"] = nd()
                        st["ACC"] = wk.f(128); st["T1"] = wk.f(128); st["DP"] = wk.f(8); st["ACCd"] = nd()
                        st["SINB"] = wk.b(128); st["SINd"] = nd()
                        st["SQO"] = st["F3"]; st["SQOd"] = st["F3d"]
                        st["RS"] = st["F2"]; st["RSd"] = st["F2d"]
                    OL, OLd, SZ, SZd = st["OL"], st["OLd"], st["SZ"], st["SZd"]
                    F1, F1d, F2, F2d = st["F1"], st["F1d"], st["F2"], st["F2d"]
                    SQ, SQd, F3, F3d, F4, F4d, F5, F5d, F6, F6d = (st[k] for k in ("SQ", "SQd", "F3", "F3d", "F4", "F4d", "F5", "F5d", "F6", "F6d"))
                    KTTf, KTTd, SC, SCd, ST, STd, SB, SBd = (st[k] for k in ("KTTf", "KTTd", "SC", "SCd", "ST", "STd", "SB", "SBd"))
                    KTT = [k_.rearrange("p (a b) -> p a b", b=128) for k_ in KTTf]
                    AGS, AGSd, AGR, AGRd, ACC, T1, DP, ACCd = (st[k] for k in ("AGS", "AGSd", "AGR", "AGRd", "ACC", "T1", "DP", "ACCd"))
                    SINB, SINd, SQO, SQOd, RS, RSd = (st[k] for k in ("SINB", "SINd", "SQO", "SQOd", "RS", "RSd"))
                    hp = h % 2
                    lbs = LBALL[:, h, l:l + 1]; oml = OML[:, h, l:l + 1]; noml = NOML[:, h, l:l + 1]; c2s = C2[:, h, l:l + 1]
                    PL = lambda fn, reads, writes: S.op("dve", fn, reads, writes)

                    ST2, KVS, KVSd = st["ST2"], st["KVS"], st["KVSd"]

                    def G(tg):
                        tok = slice(tg * 512, (tg + 1) * 512)
                        steps = []
                        for (bank, c0) in ((0, 0), (1, 128)):
                            steps.append(lambda bank=bank, c0=c0: S.group(
                                "pe", [mm(PS[bank], W[:, dc, c0:c0 + 128], hT[:, dc, tok], dc == 0, dc == 15) for dc in range(16)],
                                [Wd, hTd[tg]], [PSd[bank]]))

                        def gv():
                            fns = []
                            for ti in range(4):
                                t0 = tg * 512 + ti * 128
                                for dc in range(16):
                                    fns.append(mm(PS[3][:, ti * 128:(ti + 1) * 128], hT[:, dc, t0:t0 + 128], W[:, dc, 256:384], dc == 0, dc == 15))
                            S.group("pe", fns, [Wd, hTd[tg]], [PSd[3]])
                        steps.append(gv)
                        return steps

                    def Erest(tg):
                        par = tg % 2
                        tok = slice(tg * 512, (tg + 1) * 512)
                        QT, QTd, KT, KTd = st["QT"][par], st["QTd"][par], st["KT"][par], st["KTd"][par]
                        VTf, VTd = st["VT"][par], st["VTd"][par]
                        BM, BMd, EE, EEd = st["BM"][par], st["BMd"][par], st["EE"][par], st["EEd"][par]
                        QH, QHd = st["QH"][hp], st["QHd"][hp]
                        steps = []
                        ap = steps.append
                        ap(lambda: act(F1, PS[1], AF.Sigmoid, [PSd[1]], [F1d]))
                        ap(lambda: act(SQ, PS[0], AF.Silu, [PSd[0]], [SQd]))
                        ap(lambda: act(VTf, PS[3], AF.Copy, [PSd[3]], [VTd]))
                        ap(lambda: PL(lambda e: e.tensor_scalar(out=F5, in0=F1, scalar1=noml, scalar2=oml, op0=ALU.mult, op1=ALU.add), [F1d] + CST, [F5d]))
                        ap(lambda: V(lambda e: e.tensor_scalar(out=F1, in0=F1, scalar1=oml, scalar2=c2s, op0=ALU.mult, op1=ALU.max), CST, [F1d]))
                        ap(lambda: act(F1, F1, AF.Ln, CST, [F1d], bias=lbs, scale=1.0))
                        if tg > 0:
                            BMp, BMpd = st["BM"][1 - par], st["BMd"][1 - par]
                            ap(lambda: V(lambda e: e.tensor_copy(out=BM[:, 0:1], in_=BMp[:, 17:18]), [BMpd], [BMd]))
                        else:
                            ap(lambda: V(lambda e: e.memset(BM[:, 0:1], 0.0), [], [BMd]))
                        ap(lambda: V(lambda e: e.tensor_tensor_scan(out=F2, data0=ONESF, data1=F1, initial=BM[:, 0:1], op0=ALU.mult, op1=ALU.add),
                                     [F1d, BMd] + CST, [F2d]))
                        ap(lambda: V(lambda e: e.tensor_copy(out=BM[:, 1:17], in_=F2[:, 15:512:32]), [F2d], [BMd]))
                        ap(lambda: V(lambda e: e.tensor_copy(out=BM[:, 17:18], in_=F2[:, 511:512]), [F2d], [BMd]))
                        ap(lambda: V(lambda e: e.tensor_tensor(out=EE[:, 0:17], in0=BM[:, 1:18], in1=BM[:, 0:17], op=ALU.subtract), [BMd], [EEd]))
                        ap(lambda: act(EE[:, 0:17], EE[:, 0:17], AF.Exp, [], [EEd]))
                        ap(lambda: V(lambda e: e.tensor_tensor(out=F1.rearrange("p (a b) -> p a b", b=32), in0=F2.rearrange("p (a b) -> p a b", b=32),
                                                               in1=BM[:, 1:17].unsqueeze(2).to_broadcast([128, 16, 32]), op=ALU.subtract), [F2d, BMd], [F1d]))
                        ap(lambda: V(lambda e: e.tensor_scalar(out=F1, in0=F1, scalar1=40.0, scalar2=-40.0, op0=ALU.min, op1=ALU.max), [], [F1d]))
                        ap(lambda: act(F3, F1, AF.Exp, [F1d], [F3d]))
                        ap(lambda: act(F4, F1, AF.Exp, [F1d], [F4d], scale=-1.0))
                        ap(lambda: act(F6, F2, AF.Exp, [F2d], [F6d]))
                        ap(lambda: PL(lambda e: e.tensor_tensor(out=KT, in0=F5, in1=F4, op=ALU.mult), [F5d, F4d], [KTd]))
                        ap(lambda: PL(lambda e: e.tensor_tensor(out=QT, in0=SQ, in1=F3, op=ALU.mult), [SQd, F3d], [QTd]))
                        ap(lambda: PL(lambda e: e.tensor_tensor(out=QH[:, tok], in0=SQ, in1=F6, op=ALU.mult), [SQd, F6d], [QHd[tg]]))
                        return steps

                    def L(tg):
                        par = tg % 2
                        tok = slice(tg * 512, (tg + 1) * 512)
                        QT, QTd, KT, KTd = st["QT"][par], st["QTd"][par], st["KT"][par], st["KTd"][par]
                        VTf, VTd = st["VT"][par], st["VTd"][par]
                        VT = VTf.rearrange("p (a b) -> p a b", b=128)
                        BM, BMd, EE, EEd = st["BM"][par], st["BMd"][par], st["EE"][par], st["EEd"][par]
                        steps = []
                        ap = steps.append
                        STS = [ST, ST2]
                        if tg == 0:
                            ap(lambda: V(lambda e: e.memset(ST, 0.0), [], [STd[0]]))
                        ap(lambda: S.group("pe", [lambda e, ti=ti: e.transpose(PS7[:, ti * 128:(ti + 1) * 128], KT[:, ti * 128:(ti + 1) * 128], IDENT) for ti in range(4)],
                                           [KTd] + CST, [PSd[7]]))
                        for r4 in range(4):
                            ap(lambda r4=r4: act(KTTf[r4], PS7[:, 0:512], AF.Identity, [PSd[7]] + CST, [KTTd], scale=CVEC[:, 80 + r4:81 + r4]))
                        ap(lambda: S.group("pe", [mm(PS[4][:, ti * 128:(ti + 1) * 128], KT[:, ti * 128:(ti + 1) * 128], QT[:, ti * 128:(ti + 1) * 128], True, True) for ti in range(4)],
                                           [KTd, QTd], [PSd[4]]))
                        ap(lambda: V(lambda e: e.tensor_tensor(out=SC, in0=PS[4].rearrange("p (a b) -> p a b", b=128),
                                                               in1=CMASK.unsqueeze(1).to_broadcast([128, 4, 128]), op=ALU.mult), [PSd[4]] + CST, [SCd]))
                        ap(lambda: V(lambda e: e.tensor_scalar(out=ST, in0=ST, scalar1=EE[:, 0:1], scalar2=None, op0=ALU.mult), [EEd], [STd[0]]))
                        ap(lambda: act(SB[:, 0, :], ST, AF.Copy, [STd[0]], [SBd]))
                        for rnd in range(4):
                            bank = 5 + rnd % 2

                            def kvr(rnd=rnd, bank=bank):
                                fns = []
                                for cc in range(4):
                                    c = rnd * 4 + cc
                                    fns.append(mm(PS[bank][:, cc * 128:(cc + 1) * 128], KTT[c % 4][:, c // 4, :], VT[:, c // 4, :], True, True))
                                S.group("pe", fns, [KTTd, VTd], [PSd[bank]])
                            ap(kvr)
                            ap(lambda bank=bank, rnd=rnd: V(lambda e: e.tensor_tensor(
                                out=KVS.rearrange("p (a b) -> p a b", b=128), in0=PS[bank].rearrange("p (a b) -> p a b", b=128),
                                in1=EE[:, rnd * 4 + 1:rnd * 4 + 5].unsqueeze(2).to_broadcast([128, 4, 128]), op=ALU.mult), [PSd[bank], EEd], [KVSd]))
                            for cc in range(4):
                                c = rnd * 4 + cc
                                so, sn = STS[c % 2], STS[(c + 1) % 2]
                                ap(lambda c=c, cc=cc, so=so, sn=sn: V(lambda e: e.scalar_tensor_tensor(out=sn, in0=so, scalar=EE[:, c + 1:c + 2], in1=KVS[:, cc * 128:(cc + 1) * 128],
                                                                                                    op0=ALU.mult, op1=ALU.add), [KVSd, EEd, STd[c % 2]], [STd[(c + 1) % 2]]))
                                if c < 15:
                                    ap(lambda c=c, sn=sn: act(SB[:, c + 1, :], sn, AF.Copy, [STd[(c + 1) % 2]], [SBd]))

                        def omm():
                            fns = []
                            for c in range(16):
                                ti, r0 = c // 4, (c % 4) * 32
                                cols = slice(c * 32, (c + 1) * 32)
                                fns.append(mm(PS[2][:, cols], VT[:, ti, :], SC[:, ti, r0:r0 + 32], True, False))
                                fns.append(mm(PS[2][:, cols], SB[:, c, :], QT[:, cols], False, True))
                            S.group("pe", fns, [VTd, SCd, SBd, QTd], [PSd[2]])
                        ap(omm)
                        ap(lambda: act(OL[:, tok], PS[2], AF.Copy, [PSd[2]], [OLd[tg]]))
                        if tg == 3:
                            def agx():
                                V(lambda e: e.tensor_copy(out=AGS[:, 0:128], in_=ST), [STd[0]], [AGSd])
                                act(AGS[:, 128:130], BM[:, 17:18].to_broadcast([128, 2]), AF.Exp, [BMd], [AGSd])
                                gi, go = agin[l][h], agout[l][h]
                                gid, god = S.dep(), S.dep()
                                S.dma(sp, lambda e: e.dma_start(out=gi[:, :], in_=AGS), S.sd("agi"), reads=[AGSd], writes=[gid])
                                S.coll(pool, lambda e: e.collective_compute("AllGather", ALU.bypass, replica_groups=[list(range(NCORE))],
                                                                            ins=[gi.ap().opt()], outs=[go.ap().opt()]), S.sd("ago"), reads=[gid], writes=[god])
                                S.dma(sp, lambda e: e.dma_start(out=AGR, in_=go.ap().rearrange("(r p) f -> p r f", p=128)), S.sd("agr"), reads=[god], writes=[AGRd])
                            ap(agx)
                            for tz in range(4):
                                tkz = slice(tz * 512, (tz + 1) * 512)
                                ap(lambda tz=tz, tkz=tkz: S.group("pe", [mm(PS[3], W[:, dc, 384:512], hT[:, dc, tkz], dc == 0, dc == 15) for dc in range(16)],
                                                                  [Wd, hTd[tz]], [PSd[3]]))
                                ap(lambda tz=tz, tkz=tkz: act(SZ[:, tkz], PS[3], AF.Silu, [PSd[3]], [SZd[tz]]))
                        return steps

                    def merge(Ls, Bs=(), Cs=(), cpos=(14, 28, 42)):
                        bi = ci = 0
                        n = len(Ls)
                        for i, a_ in enumerate(Ls):
                            a_()
                            while bi < len(Bs) and (bi + 1) * n <= (i + 1) * len(Bs):
                                Bs[bi]()
                                bi += 1
                            while ci < len(Cs) and i >= cpos[min(ci, len(cpos) - 1)]:
                                Cs[ci]()
                                ci += 1
                        for k in range(bi, len(Bs)):
                            Bs[k]()
                        for k in range(ci, len(Cs)):
                            Cs[k]()

                    def P(hh):
                        QH, QHd = st["QH"][hh % 2], st["QHd"][hh % 2]
                        V(lambda e: e.memset(ACC, 0.0), [], [ACCd])
                        for j in range(8):
                            mj = CVEC[:, j:j + 1]
                            V(lambda e, j=j, mj=mj: e.tensor_scalar(out=DP[:, j:j + 1], in0=AGR[:, j, 128:129], scalar1=-1.0, scalar2=mj, op0=ALU.add, op1=ALU.mult),
                              [AGRd] + CST, [ACCd])
                            V(lambda e, j=j: e.tensor_scalar(out=T1, in0=ACC, scalar1=DP[:, j:j + 1], scalar2=None, op0=ALU.mult), [], [ACCd])
                            V(lambda e: e.tensor_tensor(out=ACC, in0=ACC, in1=T1, op=ALU.add), [], [ACCd])
                            V(lambda e, j=j, mj=mj: e.scalar_tensor_tensor(out=ACC, in0=AGR[:, j, 0:128], scalar=mj, in1=ACC, op0=ALU.mult, op1=ALU.add), [AGRd], [ACCd])
                        V(lambda e: e.tensor_copy(out=SINB, in_=ACC), [ACCd], [SINd])
                        for tg in range(4):
                            tok = slice(tg * 512, (tg + 1) * 512)
                            S.group("pe", [mm(PS[4], SINB, QH[:, tok], True, True)], [SINd, QHd[tg]], [PSd[4]])
                            V(lambda e, tok=tok: e.tensor_tensor(out=OL[:, tok], in0=PS[4], in1=OL[:, tok], op=ALU.add), [PSd[4]], [OLd[tg]])
                            act(SQO, OL[:, tok], AF.Square, [OLd[tg]], [SQOd])
                            S.group("pe", [mm(PS[5], ONESB, SQO, True, True)], [SQOd] + CST, [PSd[5]])
                            act(RS, PS[5], AF.Sqrt, [PSd[5]] + CST, [RSd], scale=1.0 / 128, bias=EPSC)
                            V(lambda e: e.reciprocal(out=RS, in_=RS), [], [RSd])
                            V(lambda e, tok=tok: e.tensor_tensor(out=OL[:, tok], in0=OL[:, tok], in1=RS, op=ALU.mult), [RSd], [OLd[tg]])
                            V(lambda e, tok=tok, hh=hh: e.scalar_tensor_tensor(out=YA[:, hh, tok], in0=OL[:, tok], scalar=HN[:, hh:hh + 1], in1=SZ[:, tok], op0=ALU.mult, op1=ALU.mult),
                              [OLd[tg], SZd[tg], lyr], [YAd[hh][tg]])

                    merge(G(0))
                    merge(Erest(0), Cs=G(1), cpos=(2, 8, 14))
                    if h > 0:
                        P(h - 1)
                    merge(L(0), Erest(1), G(2))
                    merge(L(1), Erest(2), G(3))
                    merge(L(2), Erest(3))
                    merge(L(3))
                    if h == 7:
                        P(7)
                return f

            bstate = {}

            def b_load(g, l=l):
                def f(W, Wd):
                    S.dma(pool, lambda e: e.dma_start(out=W[:, :, 0:256], in_=wrows(w_in[l, :, 4096 + g * 256:4096 + (g + 1) * 256])), Wd, writes=[Wd])
                    S.dma(pool, lambda e: e.dma_start(out=W[:, :, 256:512], in_=wrows(w_in[l, :, 5120 + g * 256:5120 + (g + 1) * 256])), Wd, writes=[Wd])
                return f

            def b_comp(g, l=l, bstate=bstate):
                def f(W, Wd):
                    nd = lambda: S.dep(snapshot=True)
                    if g == 0:
                        wk = Work([(O_WK, O_MISC)])
                        bstate["wk"] = wk
                        bstate["FIRST"] = wk.f(8 * 32).rearrange("p (a b) -> p a b", b=32)
                        bstate["HAL"] = wk.f(130)
                        bstate["SZF"] = wk.f(8 * 16).rearrange("p (a b) -> p a b", b=16)
                        bstate["VBE"] = [wk.f(528), wk.f(528)]
                        bstate["TA"] = wk.f(528)
                        bstate["TB"] = wk.f(528)
                        bstate["PBT"] = [wk.b(512), wk.b(512)]
                        for k in ("FIRSTd", "HALd", "SZFd", "TAd", "PBTd0", "PBTd1", "VBEd0", "VBEd1"):
                            bstate[k] = nd()
                    wk = bstate["wk"]
                    FIRST, HAL, SZF, VBE, TA, TB, PBT = (bstate[k] for k in ("FIRST", "HAL", "SZF", "VBE", "TA", "TB", "PBT"))
                    FIRSTd, HALd, SZFd, TAd = (bstate[k] for k in ("FIRSTd", "HALd", "SZFd", "TAd"))
                    PBTd = [bstate["PBTd0"], bstate["PBTd1"]]
                    VBEd = [bstate["VBEd0"], bstate["VBEd1"]]
                    w = WINS[g]
                    for q in range(2):
                        V(lambda e, q=q: e.memset(VBE[q][:, 0:16], 0.0), [], [VBEd[q]])
                    for tg in range(4):
                        tok = slice(tg * 512, (tg + 1) * 512)
                        for q in range(4):
                            S.group("pe", [mm(PS[q], W[:, dc, q * 128:(q + 1) * 128], hT[:, dc, tok], dc == 0, dc == 15) for dc in range(16)],
                                    [Wd, hTd[tg]], [PSd[q]])
                        for q in range(2):
                            j = g * 2 + q
                            act(VBE[q][:, 16:528], PS[q], AF.Copy, [PSd[q]], [VBEd[q]])
                            act(YB[:, j, tok], PS[2 + q], AF.Silu, [PSd[2 + q]], [YBd[j][tg]])
                            if tg == 0:
                                V(lambda e, q=q, j=j: e.tensor_copy(out=FIRST[:, j, 16:32], in_=VBE[q][:, 16:32]), [VBEd[q]], [FIRSTd])
                                V(lambda e, j=j: e.tensor_copy(out=SZF[:, j, :], in_=YB[:, j, 0:16]), [YBd[j][0]], [SZFd])
                            src = VBE[q]
                            sh = 1
                            lo = 0
                            bufs = [TA, TB]
                            bi = 0
                            while sh < w:
                                dst = bufs[bi]
                                lo += sh
                                V(lambda e, src=src, dst=dst, sh=sh, lo=lo: e.tensor_tensor(out=dst[:, lo:528], in0=src[:, lo:528], in1=src[:, lo - sh:528 - sh], op=ALU.add),
                                  [VBEd[q]], [TAd])
                                src = dst
                                bi ^= 1
                                sh *= 2
                            V(lambda e, src=src, q=q: e.scalar_tensor_tensor(out=PBT[q], in0=src[:, 16:528], scalar=1.0 / w, in1=VBE[q][:, 16:528], op0=ALU.mult, op1=ALU.subtract),
                              [TAd, VBEd[q]], [PBTd[q]])
                            if tg == 3:
                                V(lambda e, q=q, j=j: e.tensor_copy(out=HAL[:, j * 16:(j + 1) * 16], in_=VBE[q][:, 512:528]), [VBEd[q]], [HALd])
                            else:
                                V(lambda e, q=q: e.tensor_copy(out=VBE[q][:, 0:16], in_=VBE[q][:, 512:528]), [TAd, PBTd[q]], [VBEd[q]])
                        for q in range(2):
                            j = g * 2 + q
                            S.group("pe", [mm(PS[4 + q], POOLW[:, g, cq, q * 128:(q + 1) * 128], PBT[cq], cq == 0, cq == 1) for cq in range(2)],
                                    [PBTd[0], PBTd[1], POOLWd], [PSd[4 + q]])
                            V(lambda e, q=q, j=j, tok=tok: e.scalar_tensor_tensor(out=YB[:, j, tok], in0=PS[4 + q], scalar=PSC[:, j:j + 1], in1=YB[:, j, tok], op0=ALU.mult, op1=ALU.mult),
                              [PSd[4 + q], lyr], [YBd[j][tg]])
                    if g == 3:
                        AGR = wk.f(8 * 130).rearrange("p (a b) -> p a b", b=130)
                        AGRd = nd()
                        PF = wk.b(8 * 16).rearrange("p (a b) -> p a b", b=16)
                        PFd = nd()
                        FA = wk.f(8 * 32).rearrange("p (a b) -> p a b", b=32)
                        FB = wk.f(8 * 32).rearrange("p (a b) -> p a b", b=32)
                        FAd = nd()
                        gi, go = agin[l][8], agout[l][8]
                        gid, god = S.dep(), S.dep()
                        V(lambda e: e.memset(HAL[:, 128:130], 0.0), [], [HALd])
                        S.dma(sp, lambda e: e.dma_start(out=gi[:, :], in_=HAL), S.sd("agi"), reads=[HALd], writes=[gid])
                        S.coll(pool, lambda e: e.collective_compute("AllGather", ALU.bypass, replica_groups=[list(range(NCORE))],
                                                                    ins=[gi.ap().opt()], outs=[go.ap().opt()]), S.sd("ago"), reads=[gid], writes=[god])
                        S.dma(sp, lambda e: e.dma_start(out=AGR, in_=go.ap().rearrange("(r p) f -> p r f", p=128)), S.sd("agr"), reads=[god], writes=[AGRd])
                        V(lambda e: e.memset(FIRST[:, :, 0:16], 0.0), [], [FIRSTd])
                        for r in range(8):
                            V(lambda e, r=r: e.scalar_tensor_tensor(out=FIRST[:, :, 0:16], in0=AGR[:, r, 0:128].rearrange("p (a b) -> p a b", b=16),
                                                                    scalar=CVEC[:, 8 + r:9 + r], in1=FIRST[:, :, 0:16], op0=ALU.mult, op1=ALU.add),
                              [AGRd] + CST, [FIRSTd])
                        for gg in range(4):
                            ww = WINS[gg]
                            js = slice(gg * 2, gg * 2 + 2)
                            src = FIRST
                            sh = 1
                            lo = 0
                            bufs = [FA, FB]
                            bi = 0
                            while sh < ww:
                                dst = bufs[bi]
                                lo += sh
                                V(lambda e, src=src, dst=dst, sh=sh, js=js, lo=lo: e.tensor_tensor(out=dst[:, js, lo:32], in0=src[:, js, lo:32], in1=src[:, js, lo - sh:32 - sh], op=ALU.add),
                                  [FIRSTd], [FAd])
                                src = dst
                                bi ^= 1
                                sh *= 2
                            rd = CVEC[:, 16 + gg * 16:32 + gg * 16]
                            V(lambda e, src=src, js=js, rd=rd: e.tensor_tensor(out=src[:, js, 16:32], in0=src[:, js, 16:32], in1=rd.unsqueeze(1).to_broadcast([128, 2, 16]), op=ALU.mult),
                              CST, [FAd])
                            V(lambda e, src=src, js=js: e.tensor_tensor(out=PF[:, js, :], in0=src[:, js, 16:32], in1=FIRST[:, js, 16:32], op=ALU.subtract), [FAd, FIRSTd], [PFd])
                        for gg in range(4):
                            for q in range(2):
                                j = gg * 2 + q
                                S.group("pe", [mm(PS[4 + q][:, 0:16], POOLW[:, gg, cq, q * 128:(q + 1) * 128], PF[:, gg * 2 + cq, :], cq == 0, cq == 1) for cq in range(2)],
                                        [PFd, POOLWd], [PSd[4 + q]])
                                V(lambda e, q=q, j=j: e.scalar_tensor_tensor(out=YB[:, j, 0:16], in0=PS[4 + q][:, 0:16], scalar=PSC[:, j:j + 1], in1=SZF[:, j, :], op0=ALU.mult, op1=ALU.mult),
                                  [PSd[4 + q], SZFd, lyr], [YBd[j][0]])
                return f

            cstate = {}

            def c_load(dco, l=l):
                def f(W, Wd):
                    c0 = dco * 128
                    S.dma(pool, lambda e: e.dma_start(out=W[:, :, 0:128], in_=wrows(w_in[l, :, 6144 + c0:6144 + c0 + 128])), Wd, writes=[Wd])
                    S.dma(pool, lambda e: e.dma_start(out=W[:, :, 128:256], in_=wrows(w_in[l, :, 8192 + c0:8192 + c0 + 128])), Wd, writes=[Wd])
                    S.dma(pool, lambda e: e.dma_start(out=W[:, 0:8, 256:384], in_=wrows(w_pa[l, :, c0:c0 + 128])), Wd, writes=[Wd])
                    S.dma(pool, lambda e: e.dma_start(out=W[:, 0:8, 384:512], in_=wrows(w_pb[l, :, c0:c0 + 128])), Wd, writes=[Wd])
                return f

            def c_comp(dco, l=l, cstate=cstate):
                def f(W, Wd):
                    nd = lambda: S.dep(snapshot=True)
                    if dco == 0:
                        wk = Work([(O_WK, O_MISC)])
                        cstate["MG"] = [wk.b(2048), wk.b(2048)]
                        cstate["MGd"] = [nd(), nd()]
                        cstate["SA"] = wk.f(512); cstate["SB_"] = wk.f(512)
                        cstate["SAd"] = nd(); cstate["SBd"] = nd()
                    MG, MGd = cstate["MG"][dco % 2], cstate["MGd"][dco % 2]
                    SA, SBb, SAd, SBd2 = cstate["SA"], cstate["SB_"], cstate["SAd"], cstate["SBd"]
                    for tg in range(4):
                        tok = slice(tg * 512, (tg + 1) * 512)
                        S.group("pe", [mm(PS[0], W[:, dc, 0:128], hT[:, dc, tok], dc == 0, dc == 15) for dc in range(16)], [Wd, hTd[tg]], [PSd[0]])
                        S.group("pe", [mm(PS[1], W[:, dc, 128:256], hT[:, dc, tok], dc == 0, dc == 15) for dc in range(16)], [Wd, hTd[tg]], [PSd[1]])
                        S.group("pe", [mm(PS[2], W[:, kc, 256:384], YA[:, kc, tok], kc == 0, kc == 7) for kc in range(8)],
                                [Wd] + [YAd[kc][tg] for kc in range(8)], [PSd[2]])
                        S.group("pe", [mm(PS[3], W[:, kc, 384:512], YB[:, kc, tok], kc == 0, kc == 7) for kc in range(8)],
                                [Wd] + [YBd[kc][tg] for kc in range(8)], [PSd[3]])
                        act(SA, PS[0], AF.Sigmoid, [PSd[0]], [SAd])
                        act(SBb, PS[1], AF.Sigmoid, [PSd[1]], [SBd2])
                        V(lambda e: e.tensor_tensor(out=SA, in0=PS[2], in1=SA, op=ALU.mult), [PSd[2]], [SAd])
                        V(lambda e: e.tensor_tensor(out=SBb, in0=PS[3], in1=SBb, op=ALU.mult), [PSd[3]], [SBd2])
                        V(lambda e, tok=tok, MG=MG: e.tensor_tensor(out=MG[:, tok], in0=SA, in1=SBb, op=ALU.add), [SAd, SBd2], [MGd])
                    S.dma(sp, lambda e, MG=MG: e.dma_start(out=mg_dram[dco], in_=MG), S.sd("mg%d" % (dco % 2)), reads=[MGd], writes=[mgd[dco]])
                return f

            dstate = {}

            def d_load(hf, cb, l=l):
                def f(W, Wd):
                    S.dma(pool, lambda e: e.dma_start(out=W, in_=wrows(w_out[l, :, cb * 512:(cb + 1) * 512])), Wd, writes=[Wd])
                return f

            def d_comp(hf, cb, l=l, dstate=dstate, x_src=x_src):
                def f(W, Wd):
                    nd = lambda: S.dep(snapshot=True)
                    if hf == 0 and cb == 0:
                        for dc in range(16):
                            S.dma(sp, lambda e, dc=dc: e.dma_start(out=hT[:, dc, :], in_=mg_dram[dc]), S.sd("mgld"),
                                  reads=[mgd[dc]], writes=hTd)
                        wk = Work([(O_WK, O_MISC)])
                        dstate["XT"] = [wk.f(2048), wk.f(2048)]
                        dstate["XTd"] = [nd(), nd()]
                        dstate["JK"] = wk.b(512); dstate["JKd"] = nd()
                        dstate["SS"] = wk.f(64).rearrange("p (a b) -> p a b", b=4); dstate["SSd"] = nd()
                        dstate["RS"] = wk.f(16)
                        dstate["OUTd"] = [nd() for _ in range(8)]
                    XT, XTd, JK, JKd, SS, SSd, RSs, OUTd = (dstate[k] for k in ("XT", "XTd", "JK", "JKd", "SS", "SSd", "RS", "OUTd"))
                    for t8 in range(8):
                        tt = hf * 8 + t8
                        S.group("pe", [mm(PS[t8 % 4], hT[:, dc, tt * 128:(tt + 1) * 128], W[:, dc, :], dc == 0, dc == 15) for dc in range(16)],
                                [Wd] + hTd, [PSd[t8 % 4]])
                        act(OUTH[:, t8, cb * 512:(cb + 1) * 512], PS[t8 % 4], AF.Copy, [PSd[t8 % 4]], [OUTd[t8]])
                        V(lambda e, tt=tt: e.memset(SS[:, tt, cb:cb + 1], 0.0), [], [SSd])
                        act(JK, PS[t8 % 4], AF.Square, [PSd[t8 % 4]], [JKd, SSd], accum_out=SS[:, tt, cb:cb + 1])
                        if cb == 3:
                            xt, xd = XT[tt % 2], XTd[tt % 2]
                            S.dma(sp, lambda e, xt=xt, tt=tt: e.dma_start(out=xt, in_=x_src[tt * 128:(tt + 1) * 128, :]), S.sd("xt%d" % (tt % 2)), reads=[outd[tt]], writes=[xd])
                            rs = RSs[:, tt:tt + 1]
                            V(lambda e, tt=tt, rs=rs: e.reduce_sum(out=rs, in_=SS[:, tt, :], axis=AX.X), [SSd], [SSd])
                            act(rs, rs, AF.Sqrt, CST, [SSd], scale=1.0 / D, bias=EPSC)
                            V(lambda e, rs=rs: e.reciprocal(out=rs, in_=rs), [], [SSd])
                            V(lambda e, t8=t8, rs=rs: e.scalar_tensor_tensor(out=OUTH[:, t8, :], in0=OUTH[:, t8, :], scalar=rs, in1=GP, op0=ALU.mult, op1=ALU.mult),
                              [SSd, GPd], [OUTd[t8]])
                            V(lambda e, t8=t8, xt=xt: e.tensor_tensor(out=xt, in0=OUTH[:, t8, :], in1=xt, op=ALU.add), [OUTd[t8]], [xd])
                            S.dma(sp, lambda e, xt=xt, tt=tt: e.dma_start(out=out[tt * 128:(tt + 1) * 128, :], in_=xt), S.sd("xt%d" % (tt % 2)), reads=[xd], writes=[outd[tt]])
                return f

            for jb in range(8):
                tasks.append((m_load(jb), m_comp(jb)))
            tasks.append((None, p0_comp()))
            for jb in range(8, 12):
                tasks.append((m_load(jb), m_comp(jb)))
            for h in range(8):
                tasks.append((a_load(h), a_comp(h)))
            for g in range(4):
                tasks.append((b_load(g), b_comp(g)))
            for dco in range(16):
                tasks.append((c_load(dco), c_comp(dco)))
            for hf in range(2):
                for cb in range(4):
                    tasks.append((d_load(hf, cb), d_comp(hf, cb)))

        import os
        _mt = int(os.environ.get("KMAXTASK", "0"))
        if _mt:
            tasks = tasks[:_mt]
        wtasks = [i for i, t in enumerate(tasks) if t[0] is not None]
        slot_of = {ti: k % 2 for k, ti in enumerate(wtasks)}
        nxt = {wtasks[k]: wtasks[k + 1] for k in range(len(wtasks) - 1)}
        first = wtasks[0]
        tasks[first][0](WSL[slot_of[first]], WSd[slot_of[first]])
        for i, (ld, cp) in enumerate(tasks):
            if ld is not None:
                if i in nxt:
                    n = nxt[i]
                    tasks[n][0](WSL[slot_of[n]], WSd[slot_of[n]])
                try:
                    cp(WSL[slot_of[i]], WSd[slot_of[i]])
                except _Stop:
                    break
            else:
                cp(None, None)
        S.drain("sp")
        if os.environ.get("KDEBUG"):
            print({n: (len(e.ops), e.cnt) for n, e in S.eng.items()}, len(S.sems), flush=True)

        with nc.Block() as block:
            @block.tensor
            def _(e):
                for f in S.eng["pe"].ops:
                    f(e)

            @block.scalar
            def _(e):
                for f in S.eng["act"].ops:
                    f(e)

            @block.vector
            def _(e):
                for f in S.eng["dve"].ops:
                    f(e)

            @block.gpsimd
            def _(e):
                for f in S.eng["pool"].ops:
                    f(e)

            @block.sync
            def _(e):
                for f in S.eng["sp"].ops:
                    f(e)
    return nc


def host_inputs(x, c, w_ada, b_ada, norm_pre, norm_post, w_in, lower_bounds, hgrn_norm,
                pool_w, pool_scale, w_proj_a, w_proj_b, w_out):
    f = np.float32
    xs = np.ascontiguousarray(x, dtype=f).reshape(NCORE, T, D)
    fm = lambda v, n: np.ascontiguousarray(v.reshape(v.shape[0], n, 128).transpose(0, 2, 1), dtype=f)
    ident = np.eye(128, dtype=f)
    s_i = np.arange(128)[:, None]
    t_i = np.arange(128)[None, :]
    cmask = ((s_i // 32 == t_i // 32) & (t_i >= s_i)).astype(f)
    lb_fm = np.ascontiguousarray(np.asarray(lower_bounds, dtype=f).reshape(4, 8, 128).transpose(2, 1, 0)).reshape(128, 32)
    shared = {
        "w_ada": np.ascontiguousarray(w_ada, dtype=f), "w_in": np.ascontiguousarray(w_in, dtype=f),
        "w_pa": np.ascontiguousarray(w_proj_a, dtype=f), "w_pb": np.ascontiguousarray(w_proj_b, dtype=f),
        "w_out": np.ascontiguousarray(w_out, dtype=f), "pool_w": np.ascontiguousarray(pool_w, dtype=f),
        "bada_fm": fm(np.asarray(b_ada)[:, :4096], 32), "bada_gate": np.ascontiguousarray(np.asarray(b_ada)[:, 4096:], dtype=f),
        "npre_fm": fm(np.asarray(norm_pre), 16), "npost": np.ascontiguousarray(norm_post, dtype=f),
        "lb_fm": lb_fm, "hn_fm": fm(np.asarray(hgrn_norm), 8), "ps_fm": fm(np.asarray(pool_scale), 8),
        "ident": ident, "cmask": cmask,
    }
    maps = []
    for r in range(NCORE):
        b, k = r // 4, r % 4
        cvec = np.zeros((128, 96), f)
        for r4 in range(4):
            cvec[32 * r4:32 * r4 + 32, 80 + r4] = 1.0
        for j in range(NCORE):
            if j // 4 == b and j < r:
                cvec[:, j] = 1.0
            if j // 4 == b and j == r - 1:
                cvec[:, 8 + j] = 1.0
        for g, w in enumerate(WINS):
            for t in range(16):
                pos = k * T + t + 1
                cvec[:, 16 + g * 16 + t] = 1.0 / min(pos, w)
        m = dict(shared)
        m["x"] = xs[r]
        m["cfm"] = np.ascontiguousarray(np.asarray(c, dtype=f)[b].reshape(16, 128).T)
        m["cvec"] = cvec
        maps.append(m)
    return maps


_NC_CACHE = {}


def kernel(x, c, w_ada, b_ada, norm_pre, norm_post, w_in, lower_bounds, hgrn_norm,
           pool_w, pool_scale, w_proj_a, w_proj_b, w_out, _nl=4):
    maps = host_inputs(x, c, w_ada[:_nl], b_ada, norm_pre, norm_post, w_in[:_nl], lower_bounds, hgrn_norm,
                       pool_w[:_nl], pool_scale, w_proj_a[:_nl], w_proj_b[:_nl], w_out[:_nl])
    if _nl not in _NC_CACHE:
        _NC_CACHE[_nl] = build(_nl)
    nc = _NC_CACHE[_nl]
    res = run_bass_kernel_spmd(nc, maps, core_ids=list(range(NCORE)))
    o = np.stack([np.asarray(r["out"]) for r in res.results], axis=0)
    return o.reshape(2, 8192, D).astype(np.float32)
```
